# Optimizing a Trainium2 kernel written in Bass

```python
import jax, jax.numpy as jnp
from jax import lax
import numpy as np

D_MODEL = 1024
BATCH = 16
SEQ = 4096
DEPTH = 2

FOX_HEADS = 8
FOX_HEAD_DIM = 64
FOX_WIDTH = FOX_HEADS * FOX_HEAD_DIM
GDN_HEADS = 4
GDN_HEAD_DIM = 128
GDN_WIDTH = GDN_HEADS * GDN_HEAD_DIM
MIX_WIDTH = FOX_WIDTH + GDN_WIDTH
CONV_WIDTH = 4
CHUNK = 64
Q_BLOCK = 128
D_FF = 2816
EPS = 1e-6
FORGET_BIAS_INIT = 3.0
SPLIT_SIZES = (FOX_WIDTH, FOX_WIDTH, FOX_WIDTH, FOX_HEADS,
               GDN_WIDTH, GDN_WIDTH, GDN_WIDTH, GDN_HEADS, GDN_HEADS, GDN_WIDTH)
N_IN = 3 * FOX_WIDTH + FOX_HEADS + 4 * GDN_WIDTH + 2 * GDN_HEADS

kernel_name = "macaron_fox_gdn_hybrid"


def rms_norm(x, w):
    xf = x.astype(jnp.float32)
    y = xf * lax.rsqrt(jnp.mean(xf * xf, axis=-1, keepdims=True) + EPS)
    return (y * w.astype(jnp.float32)).astype(x.dtype)


def l2_norm(x):
    xf = x.astype(jnp.float32)
    return xf * lax.rsqrt(jnp.sum(xf * xf, axis=-1, keepdims=True) + EPS)


def swiglu_ffn(x, w_in, w_out):
    gate, up = jnp.split(x @ w_in, 2, axis=-1)
    return (jax.nn.silu(gate) * up) @ w_out


def causal_depthwise_conv(x, w):
    c = x.shape[-1]
    return lax.conv_general_dilated(
        x, w[:, None, :].astype(x.dtype), window_strides=(1,), padding=[(CONV_WIDTH - 1, 0)],
        dimension_numbers=("NWC", "WIO", "NWC"), feature_group_count=c)


def forgetting_attention(q, k, v, f_logit):
    seq = q.shape[2]
    scale = FOX_HEAD_DIM ** -0.5
    cum = jnp.cumsum(jax.nn.log_sigmoid(f_logit.astype(jnp.float32)), axis=-1)
    outs = []
    for blk in range(seq // Q_BLOCK):
        start, end = blk * Q_BLOCK, (blk + 1) * Q_BLOCK
        qb = q[:, :, start:end]
        kb = k[:, :, :end]
        vb = v[:, :, :end]
        s = jnp.einsum("bhqd,bhkd->bhqk", qb, kb).astype(jnp.float32) * scale
        s = s + cum[:, :, start:end, None] - cum[:, :, None, :end]
        causal = (start + jnp.arange(Q_BLOCK))[:, None] >= jnp.arange(end)[None, :]
        s = jnp.where(causal, s, -jnp.inf)
        p = jax.nn.softmax(s, axis=-1)
        outs.append(jnp.einsum("bhqk,bhkd->bhqd", p.astype(v.dtype), vb))
    return jnp.concatenate(outs, axis=2)


def gated_delta_rule_chunked(q, k, v, g, beta):
    out_dtype = v.dtype
    b, h, seq, dk = q.shape
    dv = v.shape[-1]
    n = seq // CHUNK
    q = q.astype(jnp.float32) * dk ** -0.5
    k = k.astype(jnp.float32)
    v = v.astype(jnp.float32)
    g = g.astype(jnp.float32).reshape(b, h, n, CHUNK)
    beta = beta.astype(jnp.float32).reshape(b, h, n, CHUNK)
    q = q.reshape(b, h, n, CHUNK, dk)
    k = k.reshape(b, h, n, CHUNK, dk)
    v = v.reshape(b, h, n, CHUNK, dv)

    g = jnp.cumsum(g, axis=-1)
    tri_incl = jnp.tril(jnp.ones((CHUNK, CHUNK), dtype=bool))
    tri_strict = jnp.tril(jnp.ones((CHUNK, CHUNK), dtype=bool), k=-1)
    decay = jnp.exp(jnp.where(tri_incl, g[..., :, None] - g[..., None, :], -jnp.inf))

    k_beta = k * beta[..., None]
    v_beta = v * beta[..., None]
    a_strict = jnp.where(tri_strict, jnp.einsum("bhnid,bhnjd->bhnij", k_beta, k) * decay, 0.0)
    eye = jnp.eye(CHUNK, dtype=jnp.float32)
    t_mat = lax.linalg.triangular_solve(eye + a_strict, jnp.broadcast_to(eye, a_strict.shape),
                                        left_side=True, lower=True, unit_diagonal=True)
    u = jnp.einsum("bhnij,bhnjd->bhnid", t_mat, v_beta)
    w = jnp.einsum("bhnij,bhnjd->bhnid", t_mat, k_beta * jnp.exp(g)[..., None])
    attn_intra = jnp.where(tri_incl, jnp.einsum("bhnid,bhnjd->bhnij", q, k) * decay, 0.0)
    g_last = g[..., -1]
    k_dec = k * jnp.exp(g_last[..., None] - g)[..., None]
    q_dec = q * jnp.exp(g)[..., None]

    def step(state, inp):
        qd, kd, uc, wc, ac, gl = inp
        v_new = uc - jnp.einsum("bhck,bhkv->bhcv", wc, state)
        o = jnp.einsum("bhck,bhkv->bhcv", qd, state) + jnp.einsum("bhij,bhjv->bhiv", ac, v_new)
        state = state * jnp.exp(gl)[..., None, None] + jnp.einsum("bhck,bhcv->bhkv", kd, v_new)
        return state, o

    xs = tuple(jnp.moveaxis(t, 2, 0) for t in (q_dec, k_dec, u, w, attn_intra, g_last))
    state0 = jnp.zeros((b, h, dk, dv), jnp.float32)
    _, o = lax.scan(step, state0, xs)
    o = jnp.moveaxis(o, 0, 2).reshape(b, h, seq, dv)
    return o.astype(out_dtype)


def hybrid_mixer(hn, w_in, fox_q_norm, fox_k_norm, fox_f_bias,
                 gdn_conv, gdn_a_log, gdn_dt_bias, gdn_out_norm, w_out):
    b, s, _ = hn.shape
    proj = hn @ w_in
    offsets = []
    acc = 0
    for size in SPLIT_SIZES[:-1]:
        acc += size
        offsets.append(acc)
    fq, fk, fv, ff, gq, gk, gv, ga, gb, gg = jnp.split(proj, offsets, axis=-1)

    def heads(t, n_h, d_h):
        return t.reshape(b, s, n_h, d_h).transpose(0, 2, 1, 3)

    fq = rms_norm(heads(fq, FOX_HEADS, FOX_HEAD_DIM), fox_q_norm)
    fk = rms_norm(heads(fk, FOX_HEADS, FOX_HEAD_DIM), fox_k_norm)
    fv = heads(fv, FOX_HEADS, FOX_HEAD_DIM)
    f_logit = (ff + fox_f_bias).transpose(0, 2, 1)
    y_fox = forgetting_attention(fq, fk, fv, f_logit)
    y_fox = y_fox.transpose(0, 2, 1, 3).reshape(b, s, FOX_WIDTH)

    qkv = jax.nn.silu(causal_depthwise_conv(jnp.concatenate([gq, gk, gv], axis=-1), gdn_conv))
    gq, gk, gv = jnp.split(qkv, 3, axis=-1)
    gq = l2_norm(heads(gq, GDN_HEADS, GDN_HEAD_DIM))
    gk = l2_norm(heads(gk, GDN_HEADS, GDN_HEAD_DIM))
    gv = heads(gv, GDN_HEADS, GDN_HEAD_DIM)
    beta = jax.nn.sigmoid(gb.astype(jnp.float32)).transpose(0, 2, 1)
    log_decay = (-jnp.exp(gdn_a_log.astype(jnp.float32))
                 * jax.nn.softplus(ga.astype(jnp.float32) + gdn_dt_bias.astype(jnp.float32))).transpose(0, 2, 1)
    y_gdn = gated_delta_rule_chunked(gq, gk, gv, log_decay, beta)
    y_gdn = rms_norm(y_gdn, gdn_out_norm) * jax.nn.silu(heads(gg, GDN_HEADS, GDN_HEAD_DIM))
    y_gdn = y_gdn.transpose(0, 2, 1, 3).reshape(b, s, GDN_WIDTH)

    return jnp.concatenate([y_fox, y_gdn], axis=-1) @ w_out


def setup_inputs(seed: int = 0) -> dict:
    key = jax.random.key(seed)
    ks = jax.random.split(key, 20)
    L, D = DEPTH, D_MODEL

    def normal(k, shape, scale):
        return jax.random.normal(k, shape, jnp.float32) * scale

    def gain(k, shape):
        return 1.0 + 0.1 * jax.random.normal(k, shape, jnp.float32)

    return {
        "x": normal(ks[0], (BATCH, SEQ, D), 1.0),
        "ffn1_norm": gain(ks[1], (L, D)),
        "ffn1_w_in": normal(ks[2], (L, D, 2 * D_FF), D ** -0.5),
        "ffn1_w_out": normal(ks[3], (L, D_FF, D), D_FF ** -0.5),
        "mix_norm": gain(ks[4], (L, D)),
        "w_in": normal(ks[5], (L, D, N_IN), D ** -0.5),
        "fox_q_norm": gain(ks[6], (L, FOX_HEAD_DIM)),
        "fox_k_norm": gain(ks[7], (L, FOX_HEAD_DIM)),
        "fox_f_bias": FORGET_BIAS_INIT + 0.1 * jax.random.normal(ks[8], (L, FOX_HEADS), jnp.float32),
        "gdn_conv": normal(ks[9], (L, CONV_WIDTH, 3 * GDN_WIDTH), CONV_WIDTH ** -0.5),
        "gdn_a_log": jnp.log(jax.random.uniform(ks[10], (L, GDN_HEADS), jnp.float32, 1.0, 16.0)),
        "gdn_dt_bias": jnp.log(jnp.expm1(jax.random.uniform(ks[11], (L, GDN_HEADS), jnp.float32, 0.001, 0.1))),
        "gdn_out_norm": gain(ks[12], (L, GDN_HEAD_DIM)),
        "w_out": normal(ks[13], (L, MIX_WIDTH, D), MIX_WIDTH ** -0.5),
        "ffn2_norm": gain(ks[14], (L, D)),
        "ffn2_w_in": normal(ks[15], (L, D, 2 * D_FF), D ** -0.5),
        "ffn2_w_out": normal(ks[16], (L, D_FF, D), D_FF ** -0.5),
    }


def reference(x, ffn1_norm, ffn1_w_in, ffn1_w_out, mix_norm, w_in, fox_q_norm, fox_k_norm,
              fox_f_bias, gdn_conv, gdn_a_log, gdn_dt_bias, gdn_out_norm, w_out,
              ffn2_norm, ffn2_w_in, ffn2_w_out):
    for l in range(DEPTH):
        x = x + 0.5 * swiglu_ffn(rms_norm(x, ffn1_norm[l]), ffn1_w_in[l], ffn1_w_out[l])
        x = x + hybrid_mixer(rms_norm(x, mix_norm[l]), w_in[l], fox_q_norm[l], fox_k_norm[l],
                             fox_f_bias[l], gdn_conv[l], gdn_a_log[l], gdn_dt_bias[l],
                             gdn_out_norm[l], w_out[l])
        x = x + 0.5 * swiglu_ffn(rms_norm(x, ffn2_norm[l]), ffn2_w_in[l], ffn2_w_out[l])
    return x
```

```python
import contextlib
import numpy as np
import concourse.bass as bass
import concourse.mybir as mybir
from concourse.bass_utils import run_bass_kernel_spmd

F32 = mybir.dt.float32
BF16 = mybir.dt.bfloat16
AF = mybir.ActivationFunctionType
ALU = mybir.AluOpType

P = 128
D = 1024
KC = 8
DFF = 2816
NJ = 22
NIN = 3600
G = 512
EPS = 1e-6
NEG = -1.0e30
N_CORES = 8

O_FQ, O_FK, O_FV, O_FF = 0, 512, 1024, 1536
O_GQ, O_GK, O_GV, O_GA, O_GB, O_GG = 1544, 2056, 2568, 3080, 3084, 3088

C_ID, C_U, C_L, C_ONE, C_BD, C_NMT, C_M01, C_FM = 0, 128, 256, 384, 512, 640, 768, 896
NCONST = 896 + 2048
PV_N1, PV_N2, PV_NM, PV_GQ, PV_GK, PV_CW, PV_GON, PV_FB, PV_DT, PV_AL = 0, 8, 16, 24, 25, 26, 74, 75, 76, 80
NPV = 84


def make_pvec(inputs, depth):
    pv = np.zeros((depth, 128, NPV), np.float32)
    p = np.arange(128)
    for l in range(depth):
        pv[l, :, PV_N1:PV_N1 + 8] = np.asarray(inputs["ffn1_norm"][l], np.float32).reshape(8, 128).T
        pv[l, :, PV_N2:PV_N2 + 8] = np.asarray(inputs["ffn2_norm"][l], np.float32).reshape(8, 128).T
        pv[l, :, PV_NM:PV_NM + 8] = np.asarray(inputs["mix_norm"][l], np.float32).reshape(8, 128).T
        pv[l, :, PV_GQ] = np.asarray(inputs["fox_q_norm"][l], np.float32)[p % 64]
        pv[l, :, PV_GK] = np.asarray(inputs["fox_k_norm"][l], np.float32)[p % 64]
        cw = np.asarray(inputs["gdn_conv"][l], np.float32)
        pv[l, :, PV_CW:PV_CW + 48] = cw.reshape(4, 12, 128).transpose(2, 1, 0).reshape(128, 48)
        pv[l, :, PV_GON] = np.asarray(inputs["gdn_out_norm"][l], np.float32)
        pv[l, 0:8, PV_FB] = np.asarray(inputs["fox_f_bias"][l], np.float32)
        pv[l, :, PV_DT:PV_DT + 4] = np.asarray(inputs["gdn_dt_bias"][l], np.float32)[None, :]
        pv[l, :, PV_AL:PV_AL + 4] = np.asarray(inputs["gdn_a_log"][l], np.float32)[None, :]
    return pv


def make_consts():
    c = np.zeros((128, NCONST), np.float32)
    r = np.arange(128)
    pp, ff = r[:, None], r[None, :]
    c[:, C_ID:C_ID + 128] = (pp == ff)
    c[:, C_U:C_U + 128] = (pp <= ff)
    c[:, C_L:C_L + 128] = (pp > ff)
    c[:, C_ONE:C_ONE + 128] = 1.0
    c[:, C_BD:C_BD + 128] = ((pp // 64) == (ff // 64))
    c[:, C_NMT:C_NMT + 128] = np.where(ff < pp, NEG, 0.0)
    c[:, C_M01:C_M01 + 128] = (ff > pp)
    t = np.arange(512)[None, :]
    for rr in range(4):
        c[:, C_FM + rr * 512:C_FM + (rr + 1) * 512] = np.where((rr * 128 + pp) > t, NEG, 0.0)
    return c


class Tk:
    __slots__ = ("t", "name", "w", "rs")

    def __init__(self, t, name):
        self.t = t
        self.name = name
        self.w = None
        self.rs = []

    def __getitem__(self, k):
        return self.t[k]


class Chan:
    def __init__(self, sem, name):
        self.sem = sem
        self.count = 0
        self.name = name


class FW:
    def __init__(self, nc, stack):
        self.nc = nc
        self.stack = stack
        self.engs = {}
        for nm, e in (("pe", nc.tensor), ("act", nc.scalar), ("dve", nc.vector),
                      ("pool", nc.gpsimd), ("sp", nc.sync)):
            sem = stack.enter_context(nc.semaphore("s_" + nm))
            self.engs[nm] = (e, Chan(sem, nm))
        self.waited = {nm: {} for nm in self.engs}
        self.chans = []
        self.uid = 0

    def sb(self, st, name, shape, dt):
        self.uid += 1
        t = st.enter_context(self.nc.sbuf_tensor("%s_%d" % (name, self.uid), list(shape), dt))
        return Tk(t, name)

    def ps(self, st, name, shape, dt=F32):
        self.uid += 1
        t = st.enter_context(self.nc.psum_tensor("%s_%d" % (name, self.uid), list(shape), dt))
        return Tk(t, name)

    def dma_chan(self, name):
        for c in self.chans:
            if c.name == name:
                return c
        sem = self.stack.enter_context(self.nc.semaphore("d_" + name))
        c = Chan(sem, name)
        self.chans.append(c)
        return c

    def _need(self, en, reads, writes):
        need = {}

        def add(dep):
            if dep is None:
                return
            ch, tk = dep
            if need.get(ch, 0) < tk:
                need[ch] = tk

        my = self.engs[en][1]
        for r in reads:
            add(r.w)
        for w in writes:
            add(w.w)
            for d in w.rs:
                if d[0] is my:
                    continue
                add(d)
        return need

    def _waits(self, en, need, skip_self_pe):
        eng, ch = self.engs[en]
        wd = self.waited[en]
        for c, tk in need.items():
            if skip_self_pe and c is ch:
                continue
            if wd.get(c, 0) >= tk:
                continue
            eng.wait_ge(c.sem, tk)
            wd[c] = tk

    def _record(self, dep, reads, writes):
        for w in writes:
            w.w = dep
            w.rs = []
        for r in reads:
            r.rs.append(dep)
            if len(r.rs) > 16:
                m = {}
                for c, t in r.rs:
                    if m.get(c, 0) < t:
                        m[c] = t
                r.rs = list(m.items())

    def op(self, en, fn, reads=(), writes=()):
        eng, ch = self.engs[en]
        self._waits(en, self._need(en, reads, writes), en == "pe")
        ins = fn(eng)
        ch.count += 1
        ins.then_inc(ch.sem, 1)
        self._record((ch, ch.count), reads, writes)
        return ins

    def dma(self, chan, out_ap, in_ap, reads=(), writes=(), en="sp"):
        eng, _ = self.engs[en]
        self._waits(en, self._need(en, reads, writes), False)
        ins = eng.dma_start(out=out_ap, in_=in_ap)
        chan.count += 16
        ins.then_inc(chan.sem, 16)
        self._record((chan, chan.count), reads, writes)
        return ins

    def dma_st(self, src_tk, out_ap, in_ap):
        return self.dma(self.dma_chan("o_" + src_tk.name), out_ap, in_ap, reads=[src_tk])

    def barrier(self):
        chs = [c for (_, c) in self.engs.values()] + self.chans
        for en, (eng, mych) in self.engs.items():
            wd = self.waited[en]
            for c in chs:
                if c is mych or c.count == 0:
                    continue
                if wd.get(c, 0) >= c.count:
                    continue
                eng.wait_ge(c.sem, c.count)
                wd[c] = c.count


class Stream:
    def __init__(self, fw, st, nbuf=6, look=4):
        self.fw = fw
        self.bufs = [fw.sb(st, "wbuf%d" % i, [P, 4096], BF16) for i in range(nbuf)]
        self.chs = [fw.dma_chan("wb%d" % i) for i in range(nbuf)]
        self.look = look
        self.plan = []
        self.issued = 0
        self.taken = 0

    def add(self, dram_ap, shape, parts=None):
        self.plan.append((dram_ap, shape, parts))

    def _issue(self, i):
        ap, shape, parts = self.plan[i]
        b = i % len(self.bufs)
        dst = self.view(self.bufs[b], shape)
        if parts is None:
            self.fw.dma(self.chs[b], dst, ap, writes=[self.bufs[b]])
        else:
            for src_ap, sel in parts:
                self.fw.dma(self.chs[b], sel(dst), src_ap, writes=[self.bufs[b]])

    @staticmethod
    def view(buf, shape):
        n = int(np.prod(shape[1:]))
        v = buf.t[:, 0:n]
        if len(shape) == 3:
            v = v.rearrange("p (a b) -> p a b", a=shape[1])
        elif len(shape) == 4:
            v = v.rearrange("p (a b c) -> p a b c", a=shape[1], b=shape[2])
        return v

    def next(self):
        i = self.taken
        while self.issued < min(len(self.plan), i + 1 + self.look):
            self._issue(self.issued)
            self.issued += 1
        self.taken += 1
        b = i % len(self.bufs)
        return self.bufs[b], self.view(self.bufs[b], self.plan[i][1])


def build(nc, n_seq, S, depth, dbg=False):
    T = n_seq * S
    NG = T // G
    NB = T // P
    NBS = S // P
    GPS = S // G

    def din(name, shape):
        return nc.dram_tensor(name, list(shape), F32, kind="ExternalInput").ap()

    def dscr(name, shape, dt):
        return nc.dram_tensor(name, list(shape), dt).ap()

    xT = din("xT", [KC, P, T])
    outT = nc.dram_tensor("outT", [KC, P, T], F32, kind="ExternalOutput").ap()
    consts_d = din("consts", [P, NCONST])
    pvec_d = din("pvec", [depth, P, NPV])
    w_ffn_in = {1: din("ffn1_w_in", [depth, D, 2 * DFF]), 2: din("ffn2_w_in", [depth, D, 2 * DFF])}
    w_ffn_out = {1: din("ffn1_w_out", [depth, DFF, D]), 2: din("ffn2_w_out", [depth, DFF, D])}
    w_in = din("w_in", [depth, D, NIN])
    w_out = din("w_out", [depth, D, D])

    wb_ffn_in = {f: dscr("wb_ffn%d_in" % f, [depth, D, 2 * DFF], BF16) for f in (1, 2)}
    wb_ffn_out = {f: dscr("wb_ffn%d_out" % f, [depth, DFF, D], BF16) for f in (1, 2)}
    wb_in = dscr("wb_in", [depth, D, NIN], BF16)
    wb_out = dscr("wb_out", [depth, D, D], BF16)
    XR = dscr("XR", [KC, P, T], F32)
    QF = dscr("QF", [n_seq, 8, 70, S], BF16)
    KF = dscr("KF", [n_seq, 8, 70, S], BF16)
    VF = dscr("VF", [P, NB, 520], BF16)
    GQ = dscr("GQ", [4, P, T], BF16)
    GK = dscr("GK", [4, P, T], BF16)
    GV = dscr("GV", [4, P, T], BF16)
    GGs = dscr("GGs", [4, P, T], BF16)
    GS = dscr("GS", [P, NB, 8], F32)
    Y = dscr("Y", [KC, P, T], BF16)

    with contextlib.ExitStack() as top:
        fw = FW(nc, top)
        top.enter_context(nc.Block())
        op, dma = fw.op, fw.dma

        cf = fw.sb(top, "cf", [P, C_FM], F32)
        cb = fw.sb(top, "cb", [P, NCONST], BF16)
        ch_c = fw.dma_chan("c")
        dma(ch_c, cf[:], consts_d[:, 0:C_FM], writes=[cf])
        op("dve", lambda e: e.tensor_copy(cb[:, 0:C_FM], cf[:]), reads=[cf], writes=[cb])
        with contextlib.ExitStack() as tst:
            ctmp = fw.sb(tst, "ctmp", [P, 2048], F32)
            dma(fw.dma_chan("c2"), ctmp[:], consts_d[:, C_FM:NCONST], writes=[ctmp])
            op("dve", lambda e: e.tensor_copy(cb[:, C_FM:NCONST], ctmp[:]), reads=[ctmp], writes=[cb])
            fw.barrier()
        nhalf = fw.sb(top, "nhalf", [P, 512], F32)
        op("pool", lambda e: e.memset(nhalf[:], -0.5), writes=[nhalf])
        negones = fw.sb(top, "negones", [P, 128], F32)
        op("pool", lambda e: e.memset(negones[:], -1.0), writes=[negones])
        epsc = fw.sb(top, "epsc", [P, 1], F32)
        op("pool", lambda e: e.memset(epsc[:], EPS), writes=[epsc])
        onec = fw.sb(top, "onec", [P, 1], F32)
        op("pool", lambda e: e.memset(onec[:], 1.0), writes=[onec])
        ones_bf = cb.t[:, C_ONE:C_ONE + 128]
        ident_bf = cb.t[:, C_ID:C_ID + 128]

        ch_p = fw.dma_chan("p")
        prm = {}
        pv_tiles = []
        for l in range(depth):
            pv_t = fw.sb(top, "pvec%d" % l, [P, NPV], F32)
            dma(ch_p, pv_t[:], pvec_d[l], writes=[pv_t])
            pv_tiles.append(pv_t)
        fw.barrier()
        for l in range(depth):
            pv_t = pv_tiles[l]
            d = {}

            class _V:
                def __init__(self, tk, c0, shape=None):
                    self.tk, self.c0, self.shape = tk, c0, shape
            gq = fw.sb(top, "gq%d" % l, [P, 1], F32)
            op("dve", lambda e: e.tensor_scalar(gq[:], pv_t[:, PV_GQ:PV_GQ + 1], 0.125, None, ALU.mult), reads=[pv_t], writes=[gq])
            gk = fw.sb(top, "gk%d" % l, [P, 1], F32)
            op("dve", lambda e: e.tensor_copy(gk[:], pv_t[:, PV_GK:PV_GK + 1]), reads=[pv_t], writes=[gk])
            nfb = fw.sb(top, "nfb%d" % l, [8, 1], F32)
            op("dve", lambda e: e.tensor_scalar(nfb[:], pv_t[0:8, PV_FB:PV_FB + 1], -1.0, None, ALU.mult), reads=[pv_t], writes=[nfb])
            dtb = fw.sb(top, "dtb%d" % l, [P, 4, 4], F32)
            nA = fw.sb(top, "nA%d" % l, [P, 4, 4], F32)
            eA = fw.sb(top, "eA%d" % l, [P, 4], F32)
            op("act", lambda e: e.activation(out=eA[:], in_=pv_t[:, PV_AL:PV_AL + 4], func=AF.Exp), reads=[pv_t], writes=[eA])
            for tb in range(4):
                op("dve", lambda e, tb=tb: e.tensor_copy(dtb[:, tb, :], pv_t[:, PV_DT:PV_DT + 4]), reads=[pv_t], writes=[dtb])
                op("dve", lambda e, tb=tb: e.tensor_scalar(nA[:, tb, :], eA[:], -1.0, None, ALU.mult), reads=[eA], writes=[nA])
            n1 = fw.sb(top, "n1_%d" % l, [P, KC], F32)
            n2 = fw.sb(top, "n2_%d" % l, [P, KC], F32)
            nm = fw.sb(top, "nm_%d" % l, [P, KC], F32)
            cw = fw.sb(top, "cw%d" % l, [P, 12, 4], F32)
            gon = fw.sb(top, "gon%d" % l, [P, 1], F32)
            op("dve", lambda e: e.tensor_copy(n1[:], pv_t[:, PV_N1:PV_N1 + 8]), reads=[pv_t], writes=[n1])
            op("dve", lambda e: e.tensor_copy(n2[:], pv_t[:, PV_N2:PV_N2 + 8]), reads=[pv_t], writes=[n2])
            op("dve", lambda e: e.tensor_copy(nm[:], pv_t[:, PV_NM:PV_NM + 8]), reads=[pv_t], writes=[nm])
            op("dve", lambda e: e.tensor_copy(cw[:].rearrange("p c k -> p (c k)"), pv_t[:, PV_CW:PV_CW + 48]), reads=[pv_t], writes=[cw])
            op("dve", lambda e: e.tensor_copy(gon[:], pv_t[:, PV_GON:PV_GON + 1]), reads=[pv_t], writes=[gon])
            d.update(n1=n1, n2=n2, nm=nm, gq=gq, gk=gk, nfb=nfb, cw=cw, dtb=dtb, nA=nA, gon=gon)
            prm[l] = d

        ch_w = fw.dma_chan("w")
        for l in range(depth):
            for f in (1, 2):
                for r0 in range(0, D, 256):
                    dma(ch_w, wb_ffn_in[f][l, r0:r0 + 256, :], w_ffn_in[f][l, r0:r0 + 256, :], en="pool")
                for r0 in range(0, DFF, 704):
                    dma(ch_w, wb_ffn_out[f][l, r0:r0 + 704, :], w_ffn_out[f][l, r0:r0 + 704, :], en="pool")
            for r0 in range(0, D, 512):
                dma(ch_w, wb_in[l, r0:r0 + 512, :], w_in[l, r0:r0 + 512, :], en="pool")
                dma(ch_w, wb_out[l, r0:r0 + 512, :], w_out[l, r0:r0 + 512, :], en="pool")
        with contextlib.ExitStack() as tst:
            onesrow = fw.sb(tst, "onesrow", [8, 3, G], BF16)
            op("pool", lambda e: e.memset(onesrow[:], 1.0), writes=[onesrow])
            for sq_ in range(n_seq):
                for tq in range(0, S, G):
                    dma(ch_w, QF[sq_, :, 67:70, tq:tq + G], onesrow[:], reads=[onesrow])
                    dma(ch_w, KF[sq_, :, 64:67, tq:tq + G], onesrow[:], reads=[onesrow])
            fw.barrier()

        def group_pass(src, l_mix, ffn_a, ffn_b, l_proj, final):
            with contextlib.ExitStack() as st:
                X = [fw.sb(st, "X%d" % i, [P, KC, G], F32) for i in range(2)]
                chX = [fw.dma_chan("x%d" % i) for i in range(2)]
                chY = fw.dma_chan("y")
                xn = fw.sb(st, "xn", [P, KC, G], BF16)
                Yb = xn
                H = fw.sb(st, "H", [P, NJ, G], BF16)
                sq = H
                ms = fw.sb(st, "ms", [P, G], F32)
                rstd = fw.sb(st, "rstd", [P, G], F32)
                sgt = [fw.sb(st, "sgt%d" % i, [P, G], F32) for i in range(2)]
                bank = [fw.ps(st, "bk%d" % i, [P, G], F32) for i in range(8)]
                bg, bu, bo, bm = bank[0:2], bank[2:4], bank[4:6], bank[6:8]
                chO = fw.dma_chan("o")
                stream = Stream(fw, st, nbuf=5, look=3)
                if l_proj is not None:
                    wsm = fw.sb(st, "wsm", [P, KC, 16], BF16)
                    chS = fw.dma_chan("s")
                    wv = wb_in[l_proj].rearrange("(kc p) c -> p kc c", p=P)
                    dma(chS, wsm[:, :, 0:8], wv[:, :, O_FF:O_FF + 8], writes=[wsm])
                    dma(chS, wsm[:, :, 8:16], wv[:, :, O_GA:O_GA + 8], writes=[wsm])
                    cbuf = fw.sb(st, "cbuf", [P, 12, G + 3], F32)
                    acc = [fw.sb(st, "acc%d" % i, [P, G], F32) for i in range(2)]
                    sil = [fw.sb(st, "sil%d" % i, [P, G], F32) for i in range(2)]
                    sqb = [fw.sb(st, "sqb%d" % i, [P, G], BF16) for i in range(2)]
                    ob = [fw.sb(st, "ob%d" % i, [P, G], BF16) for i in range(4)]
                    vt = [fw.sb(st, "vt%d" % i, [P, 4, 8, 65], BF16) for i in range(2)]
                    for v in vt:
                        op("pool", lambda e, v=v: e.memset(v[:], 1.0), writes=[v])
                    sm_e = fw.sb(st, "sm_e", [8, G], F32)
                    sm_l = sm_e
                    cum = fw.sb(st, "cum", [8, G], F32)
                    carry = fw.sb(st, "carry", [8, 1], F32)
                    r32 = fw.sb(st, "r32", [8, G], F32)
                    t32 = fw.sb(st, "t32", [8, G], F32)
                    ex = [fw.sb(st, "ex0", [8, 6, G], BF16)] * 2
                    onesr = fw.sb(st, "onesr", [8, G], F32)
                    op("pool", lambda e: e.memset(onesr[:], 1.0), writes=[onesr])
                    tz = fw.sb(st, "tz", [P, 4, 8], F32)
                    te = fw.sb(st, "te", [P, 4, 8], F32)
                    gs = [fw.sb(st, "gs%d" % i, [P, 4, 8], F32) for i in range(2)]

                def plan_ffn(f, l):
                    wi = wb_ffn_in[f][l].rearrange("(kc p) (gu c) -> p kc gu c", p=P, gu=2)
                    for jb in range(NJ // 2):
                        stream.add(None, [P, KC, 2, 256], parts=[
                            (wi[:, :, 0, jb * 256:(jb + 1) * 256], lambda v: v[:, :, 0, :]),
                            (wi[:, :, 1, jb * 256:(jb + 1) * 256], lambda v: v[:, :, 1, :])])
                    wo = wb_ffn_out[f][l].rearrange("(j p) c -> p j c", p=P)
                    for m in range(KC):
                        stream.add(wo[:, :, m * 128:(m + 1) * 128], [P, NJ, 128])

                for g in range(NG):
                    if l_mix is not None:
                        wm = wb_out[l_mix].rearrange("(kc p) c -> p kc c", p=P)
                        for hf in range(2):
                            stream.add(wm[:, :, hf * 512:(hf + 1) * 512], [P, KC, 512])
                    if ffn_a is not None:
                        plan_ffn(*ffn_a)
                    if ffn_b is not None:
                        plan_ffn(*ffn_b)
                    if l_proj is not None:
                        wv = wb_in[l_proj].rearrange("(kc p) c -> p kc c", p=P)
                        for o in (O_GQ, O_GK, O_GV, O_GG, O_FQ, O_FK, O_FV):
                            stream.add(wv[:, :, o:o + 512], [P, KC, 512])

                def mm(out_tk, out_ap, l_tk, l_ap, r_tk, r_ap, start, stop):
                    op("pe", lambda e: e.matmul(out_ap, l_ap, r_ap, start=start, stop=stop),
                       reads=[l_tk, r_tk], writes=[out_tk])

                def rsqrt_from(ps_tk, ps_ap, scale, npart=P, n=G):
                    op("act", lambda e: e.activation(out=ms[0:npart, 0:n], in_=ps_ap, func=AF.Identity,
                                                     bias=epsc[0:npart, :], scale=scale),
                       reads=[ps_tk, epsc], writes=[ms])
                    op("pool", lambda e: e.tensor_tensor(rstd[0:npart, 0:n], ms[0:npart, 0:n],
                                                         nhalf[0:npart, 0:n], ALU.pow),
                       reads=[ms, nhalf], writes=[rstd])

                def rmsnorm(Xt, wvec):
                    op("act", lambda e: e.activation(out=sq[:, 0:KC, :], in_=Xt[:], func=AF.Square), reads=[Xt], writes=[sq])
                    b = bm[0]
                    for kc in range(KC):
                        mm(b, b[:], cb, ones_bf, sq, sq[:, kc, :], kc == 0, kc == KC - 1)
                    rsqrt_from(b, b[:], 1.0 / D)
                    for kc in range(KC):
                        op("dve", lambda e, kc=kc: e.scalar_tensor_tensor(
                            out=xn[:, kc, :], in0=Xt[:, kc, :], scalar=wvec[:, kc:kc + 1], in1=rstd[:],
                            op0=ALU.mult, op1=ALU.mult), reads=[Xt, wvec, rstd], writes=[xn])

                def ffn(Xt, f, l):
                    rmsnorm(Xt, prm[l]["n%d" % f])
                    for jb in range(NJ // 2):
                        wtk, wp = stream.next()
                        for jj in range(2):
                            j = jb * 2 + jj
                            pg, pu = bg[j % 2], bu[j % 2]
                            for kc in range(KC):
                                mm(pg, pg[:], wtk, wp[:, kc, 0, jj * 128:(jj + 1) * 128], xn, xn[:, kc, :], kc == 0, kc == KC - 1)
                            for kc in range(KC):
                                mm(pu, pu[:], wtk, wp[:, kc, 1, jj * 128:(jj + 1) * 128], xn, xn[:, kc, :], kc == 0, kc == KC - 1)
                            sg = sgt[j % 2]
                            op("act", lambda e, pg=pg, sg=sg: e.activation(out=sg[:], in_=pg[:], func=AF.Silu),
                               reads=[pg], writes=[sg])
                            op("dve", lambda e, pu=pu, sg=sg, j=j: e.tensor_tensor(H[:, j, :], pu[:], sg[:], ALU.mult),
                               reads=[pu, sg], writes=[H])
                    for m in range(KC):
                        wtk, wp = stream.next()
                        po = bo[m % 2]
                        for j in range(NJ):
                            mm(po, po[:], wtk, wp[:, j, :], H, H[:, j, :], j == 0, j == NJ - 1)
                        op("dve", lambda e, po=po, m=m: e.scalar_tensor_tensor(
                            out=Xt[:, m, :], in0=po[:], scalar=0.5, in1=Xt[:, m, :], op0=ALU.mult, op1=ALU.add),
                            reads=[po, Xt], writes=[Xt])

                dma(chX[0], X[0][:], src[:, :, 0:G].rearrange("kc p t -> p kc t"), writes=[X[0]])
                for g in range(NG):
                    Xt = X[g % 2]
                    t0 = g * G
                    seq, ts0 = t0 // S, t0 % S
                    if g + 1 < NG:
                        dma(chX[(g + 1) % 2], X[(g + 1) % 2][:],
                            src[:, :, t0 + G:t0 + 2 * G].rearrange("kc p t -> p kc t"), writes=[X[(g + 1) % 2]])
                    if l_mix is not None:
                        dma(chY, Yb[:], Y[:, :, t0:t0 + G].rearrange("kc p t -> p kc t"), writes=[Yb])
                        for hf in range(2):
                            wtk, wp = stream.next()
                            for mm_ in range(4):
                                m = hf * 4 + mm_
                                po = bo[m % 2]
                                for kc in range(KC):
                                    mm(po, po[:], wtk, wp[:, kc, mm_ * 128:(mm_ + 1) * 128], Yb, Yb[:, kc, :], kc == 0, kc == KC - 1)
                                op("dve", lambda e, po=po, m=m: e.tensor_tensor(Xt[:, m, :], po[:], Xt[:, m, :], ALU.add),
                                   reads=[po, Xt], writes=[Xt])
                    if ffn_a is not None:
                        ffn(Xt, *ffn_a)
                    if ffn_b is not None:
                        ffn(Xt, *ffn_b)
                    if final:
                        fw.dma_st(Xt, outT[:, :, t0:t0 + G].rearrange("kc p t -> p kc t"), Xt[:])
                        continue
                    l = l_proj
                    pr = prm[l]
                    fw.dma_st(Xt, XR[:, :, t0:t0 + G].rearrange("kc p t -> p kc t"), Xt[:])
                    rmsnorm(Xt, pr["nm"])
                    blk0 = t0 // P
                    if ts0 == 0:
                        op("pool", lambda e: e.memset(cbuf[:, :, 0:3], 0.0), writes=[cbuf])
                    for gi, (dst, kind) in enumerate(((GQ, "q"), (GK, "k"), (GV, "v"), (GGs, "g"))):
                        wtk, wp = stream.next()
                        for c in range(4):
                            pb = bg[c % 2]
                            for kc in range(KC):
                                mm(pb, pb[:], wtk, wp[:, kc, c * 128:(c + 1) * 128], xn, xn[:, kc, :], kc == 0, kc == KC - 1)
                            o_t = ob[c % 4]
                            if kind == "g":
                                op("act", lambda e, pb=pb, o_t=o_t: e.activation(out=o_t[:], in_=pb[:], func=AF.Silu),
                                   reads=[pb], writes=[o_t])
                                fw.dma_st(o_t, dst[c, :, t0:t0 + G], o_t[:])
                                continue
                            ci = gi * 4 + c
                            a, s_ = acc[c % 2], sil[c % 2]
                            op("act", lambda e, pb=pb, ci=ci: e.activation(out=cbuf[:, ci, 3:3 + G], in_=pb[:], func=AF.Copy),
                               reads=[pb], writes=[cbuf])
                            op("dve", lambda e, a=a, ci=ci: e.tensor_scalar(a[:], cbuf[:, ci, 3:3 + G], pr["cw"][:, ci, 3:4], None, ALU.mult),
                               reads=[cbuf, pr["cw"]], writes=[a])
                            for k in (2, 1, 0):
                                op("dve", lambda e, a=a, ci=ci, k=k: e.scalar_tensor_tensor(
                                    out=a[:], in0=cbuf[:, ci, k:k + G], scalar=pr["cw"][:, ci, k:k + 1], in1=a[:],
                                    op0=ALU.mult, op1=ALU.add), reads=[cbuf, pr["cw"], a], writes=[a])
                            op("pool", lambda e, ci=ci: e.tensor_copy(cbuf[:, ci, 0:3], cbuf[:, ci, G:G + 3]),
                               reads=[cbuf], writes=[cbuf])
                            if kind == "v":
                                op("act", lambda e, a=a, o_t=o_t: e.activation(out=o_t[:], in_=a[:], func=AF.Silu),
                                   reads=[a], writes=[o_t])
                                fw.dma_st(o_t, dst[c, :, t0:t0 + G], o_t[:])
                                continue
                            op("act", lambda e, a=a, s_=s_: e.activation(out=s_[:], in_=a[:], func=AF.Silu), reads=[a], writes=[s_])
                            sb_ = sqb[c % 2]
                            op("act", lambda e, s_=s_, sb_=sb_: e.activation(out=sb_[:], in_=s_[:], func=AF.Square), reads=[s_], writes=[sb_])
                            pn = bm[1]
                            mm(pn, pn[:], cb, ones_bf, sb_, sb_[:], True, True)
                            rsqrt_from(pn, pn[:], 1.0)
                            qs = (128.0 ** -0.5) if kind == "q" else 1.0
                            op("dve", lambda e, s_=s_, o_t=o_t, qs=qs: e.scalar_tensor_tensor(
                                out=o_t[:], in0=s_[:], scalar=qs, in1=rstd[:], op0=ALU.mult, op1=ALU.mult),
                                reads=[s_, rstd], writes=[o_t])
                            fw.dma_st(o_t, dst[c, :, t0:t0 + G], o_t[:])
                    for dst, gvec in ((QF, pr["gq"]), (KF, pr["gk"])):
                        wtk, wp = stream.next()
                        for c in range(4):
                            pb = bg[c % 2]
                            for kc in range(KC):
                                mm(pb, pb[:], wtk, wp[:, kc, c * 128:(c + 1) * 128], xn, xn[:, kc, :], kc == 0, kc == KC - 1)
                            sb_ = sqb[c % 2]
                            op("act", lambda e, pb=pb, sb_=sb_: e.activation(out=sb_[:], in_=pb[:], func=AF.Square), reads=[pb], writes=[sb_])
                            pn = bm[1]
                            mm(pn, pn[:], cb, cb.t[:, C_BD:C_BD + 128], sb_, sb_[:], True, True)
                            rsqrt_from(pn, pn[:], 1.0 / 64)
                            o_t = ob[c % 4]
                            op("dve", lambda e, pb=pb, o_t=o_t, gvec=gvec: e.scalar_tensor_tensor(
                                out=o_t[:], in0=pb[:], scalar=gvec[:, 0:1], in1=rstd[:], op0=ALU.mult, op1=ALU.mult),
                                reads=[pb, gvec, rstd], writes=[o_t])
                            for hh in range(2):
                                fw.dma_st(o_t, dst[seq, 2 * c + hh, 0:64, ts0:ts0 + G], o_t[hh * 64:(hh + 1) * 64, :])
                    wtk, wp = stream.next()
                    vtile = vt[g % 2]
                    for tb in range(4):
                        pb = bu[tb % 2]
                        for kc in range(KC):
                            mm(pb, pb[:], xn, xn[:, kc, tb * 128:(tb + 1) * 128], wtk, wp[:, kc, :], kc == 0, kc == KC - 1)
                        op("act", lambda e, pb=pb, tb=tb: e.activation(
                            out=vtile[:, tb, :, 0:64], in_=pb[:].rearrange("p (h d) -> p h d", h=8), func=AF.Copy),
                            reads=[pb], writes=[vtile])
                    fw.dma_st(vtile, VF[:, blk0:blk0 + 4, :], vtile[:].rearrange("p a h d -> p a (h d)"))
                    pb = bm[1]
                    for kc in range(KC):
                        mm(pb, pb[0:16, :], wsm, wsm[:, kc, :], xn, xn[:, kc, :], kc == 0, kc == KC - 1)
                    op("act", lambda e: e.activation(out=sm_e[:], in_=pb[0:8, :], func=AF.Exp, bias=pr["nfb"][:], scale=-1.0),
                       reads=[pb, pr["nfb"]], writes=[sm_e])
                    op("act", lambda e: e.activation(out=sm_l[:], in_=sm_e[:], func=AF.Ln, bias=onec[0:8, :], scale=1.0),
                       reads=[sm_e, onec], writes=[sm_l])
                    if ts0 == 0:
                        op("dve", lambda e: e.memset(carry[:], 0.0), writes=[carry])
                    op("dve", lambda e: e.tensor_tensor_scan(cum[:], onesr[:], sm_l[:], carry[:], ALU.mult, ALU.subtract),
                       reads=[onesr, sm_l, carry], writes=[cum])
                    op("dve", lambda e: e.tensor_copy(carry[:], cum[:, G - 1:G]), reads=[cum], writes=[carry])
                    ext = ex[g % 2]
                    op("dve", lambda e: e.tensor_copy(ext[:, 0, :], cum[:]), reads=[cum], writes=[ext])
                    op("dve", lambda e: e.tensor_copy(t32[:], ext[:, 0, :]), reads=[ext], writes=[t32])
                    op("dve", lambda e: e.tensor_tensor(r32[:], cum[:], t32[:], ALU.subtract), reads=[cum, t32], writes=[r32])
                    op("dve", lambda e: e.tensor_copy(ext[:, 1, :], r32[:]), reads=[r32], writes=[ext])
                    op("dve", lambda e: e.tensor_copy(t32[:], ext[:, 1, :]), reads=[ext], writes=[t32])
                    op("dve", lambda e: e.tensor_tensor(r32[:], r32[:], t32[:], ALU.subtract), reads=[r32, t32], writes=[r32])
                    op("dve", lambda e: e.tensor_copy(ext[:, 2, :], r32[:]), reads=[r32], writes=[ext])
                    op("dve", lambda e: e.tensor_scalar(ext[:, 3:6, :], ext[:, 0:3, :], -1.0, None, ALU.mult), reads=[ext], writes=[ext])
                    fw.dma_st(ext, QF[seq, :, 64:67, ts0:ts0 + G], ext[:, 0:3, :])
                    fw.dma_st(ext, KF[seq, :, 67:70, ts0:ts0 + G], ext[:, 3:6, :])
                    pb = bm[0]
                    for tb in range(4):
                        for kc in range(KC):
                            mm(pb, pb[:, tb * 16:tb * 16 + 16], xn, xn[:, kc, tb * 128:(tb + 1) * 128], wsm, wsm[:, kc, :], kc == 0, kc == KC - 1)
                    pv = pb[:, 0:64].rearrange("p (a c) -> p a c", a=4)
                    gst = gs[g % 2]
                    op("dve", lambda e: e.tensor_tensor(tz[:, :, 0:4], pv[:, :, 8:12], pr["dtb"][:], ALU.add), reads=[pb, pr["dtb"]], writes=[tz])
                    op("dve", lambda e: e.tensor_scalar(tz[:, :, 4:8], pv[:, :, 12:16], -1.0, None, ALU.mult), reads=[pb], writes=[tz])
                    op("act", lambda e: e.activation(out=te[:], in_=tz[:], func=AF.Exp), reads=[tz], writes=[te])
                    op("act", lambda e: e.activation(out=tz[:, :, 0:4], in_=te[:, :, 0:4], func=AF.Ln, bias=onec[:], scale=1.0),
                       reads=[te, onec], writes=[tz])
                    op("dve", lambda e: e.tensor_tensor(gst[:, :, 0:4], tz[:, :, 0:4], pr["nA"][:], ALU.mult), reads=[tz, pr["nA"]], writes=[gst])
                    op("dve", lambda e: e.tensor_scalar(te[:, :, 4:8], te[:, :, 4:8], 1.0, None, ALU.add), reads=[te], writes=[te])
                    op("dve", lambda e: e.reciprocal(gst[:, :, 4:8], te[:, :, 4:8]), reads=[te], writes=[gst])
                    fw.dma_st(gst, GS[:, blk0:blk0 + 4, :], gst[:])
            fw.barrier()

        def fox_phase():
            with contextlib.ExitStack() as st:
                Kt = [fw.sb(st, "Kt%d" % i, [P, S], BF16) for i in range(2)]
                Qt = [fw.sb(st, "Qt%d" % i, [P, S], BF16) for i in range(2)]
                chK = [fw.dma_chan("k%d" % i) for i in range(2)]
                chQ = [fw.dma_chan("q%d" % i) for i in range(2)]
                Vt = fw.sb(st, "Vt", [P, NBS, 520], BF16)
                chV = fw.dma_chan("v")
                pT = [fw.sb(st, "pT%d" % i, [P, G], BF16) for i in range(3)]
                bS = [fw.ps(st, "bS%d" % i, [P, G], F32) for i in range(3)]
                bO = [fw.ps(st, "bO%d" % i, [P, G], F32) for i in range(2)]
                bB = fw.ps(st, "bB", [P, G], F32)
                rc = fw.sb(st, "rc", [P, G], F32)
                bc = fw.sb(st, "bc", [64, G], F32)
                yo = [fw.sb(st, "yo%d" % i, [64, S], BF16) for i in range(2)]
                chO = fw.dma_chan("fo")
                fm = cb.t[:, C_FM:C_FM + 2048].rearrange("p (r t) -> p r t", r=4)
                cnt = 0
                hi = 0
                for seq in range(n_seq):
                    dma(chV, Vt[:], VF[:, seq * NBS:(seq + 1) * NBS, :], writes=[Vt])
                    Vv = Vt.t[:].rearrange("p a (h d) -> p a h d", h=8)
                    for h in range(8):
                        K_, Q_ = Kt[hi % 2], Qt[hi % 2]
                        dma(chK[hi % 2], K_[0:70, :], KF[seq, h], writes=[K_])
                        dma(chQ[hi % 2], Q_[0:70, :], QF[seq, h], writes=[Q_])
                        yt = yo[hi % 2]
                        hi += 1
                        for i in range(S // G):
                            po = bO[i % 2]
                            nj = 4 * (i + 1)
                            for j in range(nj):
                                ps_ = bS[cnt % 3]
                                pt = pT[cnt % 3]
                                cnt += 1
                                diag = j >= 4 * i
                                op("pe", lambda e, ps_=ps_, j=j, i=i, diag=diag, K_=K_, Q_=Q_: e.matmul(
                                    ps_[:], K_[0:70, j * 128:(j + 1) * 128], Q_[0:70, i * G:(i + 1) * G],
                                    start=True, stop=not diag), reads=[K_, Q_], writes=[ps_])
                                if diag:
                                    r = j - 4 * i
                                    op("pe", lambda e, ps_=ps_, r=r: e.matmul(ps_[:], ident_bf, fm[:, r, :], start=False, stop=True),
                                       reads=[cb], writes=[ps_])
                                op("act", lambda e, ps_=ps_, pt=pt: e.activation(out=pt[:], in_=ps_[:], func=AF.Exp),
                                   reads=[ps_], writes=[pt])
                                op("pe", lambda e, po=po, j=j, h=h, pt=pt, nj=nj: e.matmul(
                                    po[0:65, :], Vv[:, j, h, :], pt[:], start=(j == 0), stop=(j == nj - 1)),
                                    reads=[Vt, pt], writes=[po])
                            op("dve", lambda e, po=po: e.reciprocal(rc[64:65, :], po[64:65, :]), reads=[po], writes=[rc])
                            op("pe", lambda e: e.matmul(bB[0:64, :], cf[64:65, C_ONE:C_ONE + 64], rc[64:65, :], start=True, stop=True),
                               reads=[cf, rc], writes=[bB])
                            op("act", lambda e: e.activation(out=bc[:], in_=bB[0:64, :], func=AF.Copy), reads=[bB], writes=[bc])
                            op("dve", lambda e, po=po, yt=yt, i=i: e.tensor_tensor(yt[:, i * G:(i + 1) * G], po[0:64, :], bc[:], ALU.mult),
                               reads=[po, bc], writes=[yt])
                        fw.dma_st(yt, Y[h // 2, (h % 2) * 64:(h % 2) * 64 + 64, seq * S:(seq + 1) * S], yt[:])
            fw.barrier()

        def gdn_phase(l):
            pr = prm[l]
            with contextlib.ExitStack() as st:
                def sbt(name, shape, dt):
                    return fw.sb(st, name, shape, dt)
                qkv = [[sbt("%s%d" % (nm, i), [P, 4, G], BF16) for nm in ("gq", "gk", "gv", "gg")] for i in range(2)]
                chq = [fw.dma_chan("g%d" % i) for i in range(2)]
                gsb = sbt("gsb", [P, NBS, 8], F32)
                chg = fw.dma_chan("gs")
                U4 = sbt("U4", [P, 4, 128], F32)
                id4 = sbt("id4", [P, 4, 128], BF16)
                nmt4 = sbt("nmt4", [P, 4, 128], BF16)
                m014 = sbt("m014", [P, 4, 128], BF16)
                for h in range(4):
                    op("dve", lambda e, h=h: e.tensor_copy(U4[:, h, :], cf[:, C_U:C_U + 128]), reads=[cf], writes=[U4])
                    op("dve", lambda e, h=h: e.tensor_copy(id4[:, h, :], cb[:, C_ID:C_ID + 128]), reads=[cb], writes=[id4])
                    op("dve", lambda e, h=h: e.tensor_copy(nmt4[:, h, :], cb[:, C_NMT:C_NMT + 128]), reads=[cb], writes=[nmt4])
                    op("dve", lambda e, h=h: e.tensor_copy(m014[:, h, :], cb[:, C_M01:C_M01 + 128]), reads=[cb], writes=[m014])
                ng = sbt("ng", [P, 4], F32)
                nbeta = sbt("nbeta", [P, 4], F32)
                NG1 = sbt("NG1", [P, 4, 128], F32)
                esm = sbt("esm", [P, 12], F32)
                E = sbt("E", [P, 4, 128], BF16)
                qdT = sbt("qdT", [P, 4, 128], BF16)
                DT = sbt("DT", [P, 4, 128], BF16)
                DTs = sbt("DTs", [P, 4, 128], BF16)
                attnT = sbt("attnT", [P, 4, 128], BF16)
                Zc = [sbt("Zc%d" % i, [P, 4, 128], BF16) for i in range(2)]
                ZAc = [sbt("ZAc%d" % i, [P, 4, 128], BF16) for i in range(2)]
                Pc = [sbt("Pc%d" % i, [P, 4, 128], BF16) for i in range(2)]
                kg = sbt("kg", [P, 4, 128], BF16)
                kd = sbt("kd", [P, 4, 128], BF16)
                vtk = sbt("vtk", [P, 4, 128], BF16)
                ktk = sbt("ktk", [P, 4, 128], BF16)
                nwT = sbt("nwT", [P, 4, 128], BF16)
                vn = sbt("vn", [P, 4, 128], BF16)
                S32 = sbt("S32", [P, 4, 128], F32)
                Sb = sbt("Sb", [P, 4, 128], BF16)
                osq = sbt("osq", [P, 4, 128], BF16)
                o32 = sbt("o32", [P, 4, 128], F32)
                msn = sbt("msn", [P, 512], F32)
                rsn = sbt("rsn", [P, 512], F32)
                y1 = sbt("y1", [P, 4, 128], F32)
                ybuf = [sbt("ybuf%d" % i, [P, 4, G], BF16) for i in range(2)]
                chO = fw.dma_chan("go")
                bT = fw.ps(st, "bT", [P, 512], BF16)
                bA, bB_, bC, bD, bE, bF, bG_ = [fw.ps(st, "gb%d" % i, [P, 512], F32) for i in range(7)]
                ones_f = cf.t[:, C_ONE:C_ONE + 128]

                def v4(tk):
                    return tk.t[:].rearrange("p (h i) -> p h i", h=4)

                def mmh(out_tk, l_tk, l_fn, r_tk, r_fn, grp=True):
                    for h in range(4):
                        op("pe", lambda e, h=h: e.matmul(v4(out_tk)[:, h, :], l_fn(h), r_fn(h), start=True, stop=True),
                           reads=[l_tk, r_tk], writes=[out_tk])

                import os
                gstep = int(os.environ.get('K_GSTEP', '99'))
                gi = 0
                for seq in range(n_seq):
                    dma(chg, gsb[:], GS[:, seq * NBS:(seq + 1) * NBS, :], writes=[gsb])
                    op("dve", lambda e: e.memset(S32[:], 0.0), writes=[S32])
                    op("dve", lambda e: e.memset(Sb[:], 0.0), writes=[Sb])
                    for c in range(NBS):
                        tok0 = seq * S + c * P
                        if c % 4 == 0:
                            cur = qkv[gi % 2]
                            yb = ybuf[gi % 2]
                            for t_, src_ in zip(cur, (GQ, GK, GV, GGs)):
                                dma(chq[gi % 2], t_[:], src_[:, :, tok0:tok0 + G].rearrange("h p t -> p h t"), writes=[t_])
                            for t_ in cur:
                                t_.w = (chq[gi % 2], chq[gi % 2].count)
                            gi += 1
                        qT_, kT_, vT_, gg_ = cur
                        co = (c % 4) * P
                        g_ap = gsb[:, c, 0:4]
                        b_ap = gsb[:, c, 4:8]
                        op("dve", lambda e: e.tensor_scalar(ng[:], g_ap, -1.0, None, ALU.mult), reads=[gsb], writes=[ng])
                        op("dve", lambda e: e.tensor_scalar(nbeta[:], b_ap, -1.0, None, ALU.mult), reads=[gsb], writes=[nbeta])
                        op("dve", lambda e: e.tensor_tensor(NG1[:], U4[:], ng[:].unsqueeze(2).to_broadcast([P, 4, 128]), ALU.mult),
                           reads=[U4, ng], writes=[NG1])
                        op("pe", lambda e: e.matmul(bA[:, 0:4], cf[:, C_U:C_U + 128], g_ap, start=True, stop=True), reads=[cf, gsb], writes=[bA])
                        op("pe", lambda e: e.matmul(bA[:, 4:8], cf[:, C_L:C_L + 128], g_ap, start=True, stop=True), reads=[cf, gsb], writes=[bA])
                        op("pe", lambda e: e.matmul(bA[:, 8:12], ones_f, g_ap, start=True, stop=True), reads=[cf, gsb], writes=[bA])
                        op("act", lambda e: e.activation(out=esm[:], in_=bA[:, 0:12], func=AF.Exp), reads=[bA], writes=[esm])
                        if gstep < 2:
                            continue
                        NG1f = NG1.t[:].rearrange("p h i -> p (h i)")
                        op("pe", lambda e: e.matmul(bB_[:], negones[:], NG1f, start=True, stop=True), reads=[negones, NG1], writes=[bB_])
                        op("act", lambda e: e.activation(out=E[:].rearrange("p h i -> p (h i)"), in_=bB_[:], func=AF.Exp), reads=[bB_], writes=[E])
                        op("dve", lambda e: e.tensor_tensor(qdT[:], qT_[:, :, co:co + P], E[:], ALU.mult), reads=[qT_, E], writes=[qdT])
                        if gstep < 3:
                            continue
                        op("pe", lambda e: e.matmul(bC[:], negones[:], NG1f, start=True, stop=False), reads=[negones, NG1], writes=[bC])
                        for h in range(4):
                            op("pe", lambda e, h=h: e.matmul(v4(bC)[:, h, :], NG1[:, h, :], ones_f, start=False, stop=False),
                               reads=[NG1, cf], writes=[bC])
                        op("pe", lambda e: e.matmul(bC[:], ident_bf, nmt4[:].rearrange("p h i -> p (h i)"), start=False, stop=True),
                           reads=[cb, nmt4], writes=[bC])
                        op("act", lambda e: e.activation(out=DT[:].rearrange("p h i -> p (h i)"), in_=bC[:], func=AF.Exp), reads=[bC], writes=[DT])
                        op("pool", lambda e: e.tensor_tensor(DTs[:], DT[:], m014[:], ALU.mult), reads=[DT, m014], writes=[DTs])
                        if gstep < 4:
                            continue
                        mmh(bD, kT_, lambda h: kT_[:, h, co:co + P], kT_, lambda h: kT_[:, h, co:co + P])
                        mmh(bE, kT_, lambda h: kT_[:, h, co:co + P], qT_, lambda h: qT_[:, h, co:co + P])
                        Z, ZA, Pm = Zc[0], ZAc[0], Pc[0]
                        for h in range(4):
                            op("dve", lambda e, h=h, Z=Z: e.scalar_tensor_tensor(
                                out=Z[:, h, :], in0=v4(bD)[:, h, :], scalar=nbeta[:, h:h + 1], in1=DTs[:, h, :],
                                op0=ALU.mult, op1=ALU.mult), reads=[bD, nbeta, DTs], writes=[Z])
                        op("dve", lambda e: e.tensor_tensor(attnT[:].rearrange("p h i -> p (h i)"), bE[:], DT[:].rearrange("p h i -> p (h i)"), ALU.mult),
                           reads=[bE, DT], writes=[attnT])
                        op("pool", lambda e, Z=Z, Pm=Pm: e.tensor_tensor(Pm[:], Z[:], id4[:], ALU.add), reads=[Z, id4], writes=[Pm])
                        if gstep < 5:
                            continue
                        bTv = bT.t[:, 0:512].rearrange("p (h i) -> p h i", h=4)
                        for h in range(4):
                            op("pe", lambda e, h=h, Z=Z: e.transpose(bTv[:, h, :], Z[:, h, :], ident_bf), reads=[Z, cb], writes=[bT])
                        op("act", lambda e, ZA=ZA: e.activation(out=ZA[:], in_=bTv, func=AF.Copy), reads=[bT], writes=[ZA])
                        if gstep < 6:
                            continue
                        for m in range(1, 7):
                            Zn, ZAn, Pn = Zc[m % 2], ZAc[m % 2], Pc[m % 2]
                            mmh(bF, Z, lambda h, Z=Z: Z[:, h, :], ZA, lambda h, ZA=ZA: ZA[:, h, :])
                            op("act", lambda e, ZAn=ZAn: e.activation(out=ZAn[:].rearrange("p h i -> p (h i)"), in_=bF[:], func=AF.Copy),
                               reads=[bF], writes=[ZAn])
                            if m <= 5:
                                mmh(bG_, ZA, lambda h, ZA=ZA: ZA[:, h, :], Z, lambda h, Z=Z: Z[:, h, :])
                                op("dve", lambda e, Zn=Zn: e.tensor_copy(Zn[:].rearrange("p h i -> p (h i)"), bG_[:]), reads=[bG_], writes=[Zn])
                            mmh(bD, ZAn, lambda h, ZAn=ZAn: ZAn[:, h, :], Pm, lambda h, Pm=Pm: Pm[:, h, :])
                            op("dve", lambda e, Pn=Pn, Pm=Pm: e.tensor_tensor(Pn[:].rearrange("p h i -> p (h i)"), bD[:], Pm[:].rearrange("p h i -> p (h i)"), ALU.add),
                               reads=[bD, Pm], writes=[Pn])
                            Z, ZA, Pm = Zn, ZAn, Pn
                        Tt = Pm
                        if gstep < 7:
                            continue
                        bTk = bT.t[:, 0:512].rearrange("p (h i) -> p h i", h=4)
                        for h in range(4):
                            op("pe", lambda e, h=h: e.transpose(bTk[:, h, :], kT_[:, h, co:co + P], ident_bf), reads=[kT_, cb], writes=[bT])
                        op("act", lambda e: e.activation(out=ktk[:], in_=bTk, func=AF.Copy), reads=[bT], writes=[ktk])
                        for h in range(4):
                            op("pe", lambda e, h=h: e.transpose(bTk[:, h, :], vT_[:, h, co:co + P], ident_bf), reads=[vT_, cb], writes=[bT])
                        op("act", lambda e: e.activation(out=vtk[:], in_=bTk, func=AF.Copy), reads=[bT], writes=[vtk])
                        op("dve", lambda e: e.tensor_tensor(kg[:], ktk[:], esm[:, 0:4].unsqueeze(2).to_broadcast([P, 4, 128]), ALU.mult),
                           reads=[ktk, esm], writes=[kg])
                        op("dve", lambda e: e.tensor_tensor(kd[:], ktk[:], esm[:, 4:8].unsqueeze(2).to_broadcast([P, 4, 128]), ALU.mult),
                           reads=[ktk, esm], writes=[kd])
                        if gstep < 8:
                            continue
                        mmh(bB_, kg, lambda h: kg[:, h, :], Tt, lambda h, Tt=Tt: Tt[:, h, :])
                        op("act", lambda e: e.activation(out=nwT[:].rearrange("p h i -> p (h i)"), in_=bB_[:], func=AF.Identity, scale=-1.0),
                           reads=[bB_], writes=[nwT])
                        if gstep < 9:
                            continue
                        for h in range(4):
                            op("pe", lambda e, h=h, Tt=Tt: e.matmul(v4(bC)[:, h, :], Tt[:, h, :], vtk[:, h, :], start=True, stop=False),
                               reads=[Tt, vtk], writes=[bC])
                            op("pe", lambda e, h=h: e.matmul(v4(bC)[:, h, :], nwT[:, h, :], Sb[:, h, :], start=False, stop=True),
                               reads=[nwT, Sb], writes=[bC])
                        op("dve", lambda e: e.tensor_tensor(vn[:], v4(bC), b_ap.unsqueeze(2).to_broadcast([P, 4, 128]), ALU.mult),
                           reads=[bC, gsb], writes=[vn])
                        for h in range(4):
                            op("pe", lambda e, h=h: e.matmul(v4(bE)[:, h, :], Sb[:, h, :], qdT[:, h, :], start=True, stop=False),
                               reads=[Sb, qdT], writes=[bE])
                            op("pe", lambda e, h=h: e.matmul(v4(bE)[:, h, :], vn[:, h, :], attnT[:, h, :], start=False, stop=True),
                               reads=[vn, attnT], writes=[bE])
                        mmh(bF, kd, lambda h: kd[:, h, :], vn, lambda h: vn[:, h, :])
                        op("dve", lambda e: e.tensor_tensor(S32[:], S32[:], esm[:, 8:12].unsqueeze(2).to_broadcast([P, 4, 128]), ALU.mult),
                           reads=[S32, esm], writes=[S32])
                        op("dve", lambda e: e.tensor_tensor(S32[:].rearrange("p h i -> p (h i)"), bF[:], S32[:].rearrange("p h i -> p (h i)"), ALU.add),
                           reads=[bF, S32], writes=[S32])
                        op("act", lambda e: e.activation(out=Sb[:], in_=S32[:], func=AF.Copy), reads=[S32], writes=[Sb])
                        if gstep < 10:
                            continue
                        op("act", lambda e: e.activation(out=osq[:].rearrange("p h i -> p (h i)"), in_=bE[:], func=AF.Square), reads=[bE], writes=[osq])
                        op("act", lambda e: e.activation(out=o32[:].rearrange("p h i -> p (h i)"), in_=bE[:], func=AF.Copy), reads=[bE], writes=[o32])
                        op("pe", lambda e: e.matmul(bG_[:], ones_bf, osq[:].rearrange("p h i -> p (h i)"), start=True, stop=True), reads=[cb, osq], writes=[bG_])
                        op("act", lambda e: e.activation(out=msn[:], in_=bG_[:], func=AF.Identity, bias=epsc[:], scale=1.0 / 128),
                           reads=[bG_, epsc], writes=[msn])
                        op("pool", lambda e: e.tensor_tensor(rsn[:], msn[:], nhalf[:], ALU.pow), reads=[msn, nhalf], writes=[rsn])
                        op("dve", lambda e: e.scalar_tensor_tensor(out=y1[:].rearrange("p h i -> p (h i)"), in0=o32[:].rearrange("p h i -> p (h i)"),
                                                                   scalar=pr["gon"][:, 0:1], in1=rsn[:], op0=ALU.mult, op1=ALU.mult),
                           reads=[o32, pr["gon"], rsn], writes=[y1])
                        op("dve", lambda e, yb=yb: e.tensor_tensor(yb[:, :, co:co + P], y1[:], gg_[:, :, co:co + P], ALU.mult),
                           reads=[y1, gg_], writes=[yb])
                        if c % 4 == 3:
                            tb0 = tok0 - 3 * P
                            fw.dma_st(yb, Y[4:8, :, tb0:tb0 + G].rearrange("h p t -> p h t"), yb[:])
            fw.barrier()

        import os
        stop = int(os.environ.get("K_STOP", "99"))
        for l in range(depth):
            if stop <= 0:
                break
            if l == 0:
                group_pass(xT, None, None, (1, 0), 0, False)
            else:
                group_pass(XR, l - 1, (2, l - 1), (1, l), l, False)
            if stop <= 1:
                break
            fox_phase()
            if stop <= 2:
                break
            gdn_phase(l)
        if stop > 3:
            group_pass(XR, depth - 1, (2, depth - 1), None, None, True)
    return nc


def host_inputs(inputs, n_cores, n_seq, S):
    consts = make_consts()
    in_maps = []
    x = np.asarray(inputs["x"], np.float32)
    depth = int(np.asarray(inputs["w_in"]).shape[0])
    big = ("ffn1_w_in", "ffn1_w_out", "ffn2_w_in", "ffn2_w_out", "w_in", "w_out")
    shared = {k: np.ascontiguousarray(np.asarray(inputs[k], np.float32)) for k in big}
    shared["pvec"] = make_pvec(inputs, depth)
    for c in range(n_cores):
        xs = x[c * n_seq:(c + 1) * n_seq].reshape(n_seq * S, D)
        xT = np.ascontiguousarray(xs.T).reshape(KC, P, n_seq * S)
        m = {"xT": xT, "consts": consts}
        m.update(shared)
        in_maps.append(m)
    return in_maps


def kernel(**inputs):
    x = np.asarray(inputs["x"])
    B, S, _ = x.shape
    depth = int(np.asarray(inputs["w_in"]).shape[0])
    n_seq = B // N_CORES
    nc = bass.Bass("TRN2", target_bir_lowering=False)
    build(nc, n_seq, S, depth)
    in_maps = host_inputs(inputs, N_CORES, n_seq, S)
    res = run_bass_kernel_spmd(nc, in_maps, core_ids=list(range(N_CORES)))
    out = np.empty((B, S, D), np.float32)
    for c in range(N_CORES):
        oT = np.asarray(res.results[c]["outT"]).reshape(D, n_seq * S)
        out[c * n_seq:(c + 1) * n_seq] = oT.T.reshape(n_seq, S, D)
    return out
```

```python
import contextlib
import os
import numpy as np
import concourse.bass as bass
import concourse.mybir as mybir
from concourse.bass_utils import run_bass_kernel_spmd

F32 = mybir.dt.float32
BF16 = mybir.dt.bfloat16
AF = mybir.ActivationFunctionType
ALU = mybir.AluOpType

P = 128
D = 1024
KC = 8
DFF = 2816
NJ = 22
NIN = 3600
G = 512
EPS = 1e-6
NEG = -1.0e30
N_CORES = 8

O_FQ, O_FK, O_FV, O_FF = 0, 512, 1024, 1536
O_GQ, O_GK, O_GV, O_GA, O_GB, O_GG = 1544, 2056, 2568, 3080, 3084, 3088

C_ID, C_U, C_L, C_ONE, C_BD, C_NMT, C_M01, C_FM = 0, 128, 256, 384, 512, 640, 768, 896
NCONST = 896 + 2048
PV_N1, PV_N2, PV_NM, PV_GQ, PV_GK, PV_CW, PV_GON, PV_FB, PV_DT, PV_AL = 0, 8, 16, 24, 25, 26, 74, 75, 76, 80
NPV = 84


def make_pvec(inputs, depth):
    pv = np.zeros((depth, 128, NPV), np.float32)
    p = np.arange(128)
    for l in range(depth):
        pv[l, :, PV_N1:PV_N1 + 8] = np.asarray(inputs["ffn1_norm"][l], np.float32).reshape(8, 128).T
        pv[l, :, PV_N2:PV_N2 + 8] = np.asarray(inputs["ffn2_norm"][l], np.float32).reshape(8, 128).T
        pv[l, :, PV_NM:PV_NM + 8] = np.asarray(inputs["mix_norm"][l], np.float32).reshape(8, 128).T
        pv[l, :, PV_GQ] = np.asarray(inputs["fox_q_norm"][l], np.float32)[p % 64]
        pv[l, :, PV_GK] = np.asarray(inputs["fox_k_norm"][l], np.float32)[p % 64]
        cw = np.asarray(inputs["gdn_conv"][l], np.float32)
        pv[l, :, PV_CW:PV_CW + 48] = cw.reshape(4, 12, 128).transpose(2, 1, 0).reshape(128, 48)
        pv[l, :, PV_GON] = np.asarray(inputs["gdn_out_norm"][l], np.float32)
        pv[l, 0:8, PV_FB] = np.asarray(inputs["fox_f_bias"][l], np.float32)
        pv[l, :, PV_DT:PV_DT + 4] = np.asarray(inputs["gdn_dt_bias"][l], np.float32)[None, :]
        pv[l, :, PV_AL:PV_AL + 4] = np.asarray(inputs["gdn_a_log"][l], np.float32)[None, :]
    return pv


def make_consts():
    c = np.zeros((128, NCONST), np.float32)
    r = np.arange(128)
    pp, ff = r[:, None], r[None, :]
    c[:, C_ID:C_ID + 128] = (pp == ff)
    c[:, C_U:C_U + 128] = (pp <= ff)
    c[:, C_L:C_L + 128] = (pp > ff)
    c[:, C_ONE:C_ONE + 128] = 1.0
    c[:, C_BD:C_BD + 128] = ((pp // 64) == (ff // 64))
    c[:, C_NMT:C_NMT + 128] = np.where(ff < pp, NEG, 0.0)
    c[:, C_M01:C_M01 + 128] = (ff > pp)
    t = np.arange(512)[None, :]
    for rr in range(4):
        c[:, C_FM + rr * 512:C_FM + (rr + 1) * 512] = np.where((rr * 128 + pp) > t, NEG, 0.0)
    return c


class Tk:
    __slots__ = ("t", "name", "w", "rs")

    def __init__(self, t, name):
        self.t = t
        self.name = name
        self.w = None
        self.rs = []

    def __getitem__(self, k):
        return self.t[k]


class Chan:
    def __init__(self, sem, name):
        self.sem = sem
        self.count = 0
        self.name = name


class FW:
    def __init__(self, nc, stack):
        self.nc = nc
        self.stack = stack
        self.engs = {}
        for nm, e in (("pe", nc.tensor), ("act", nc.scalar), ("dve", nc.vector),
                      ("pool", nc.gpsimd), ("sp", nc.sync)):
            sem = stack.enter_context(nc.semaphore("s_" + nm))
            self.engs[nm] = (e, Chan(sem, nm))
        self.waited = {nm: {} for nm in self.engs}
        self.chans = []
        self.uid = 0

    def sb(self, st, name, shape, dt):
        self.uid += 1
        t = st.enter_context(self.nc.sbuf_tensor("%s_%d" % (name, self.uid), list(shape), dt))
        return Tk(t, name)

    def ps(self, st, name, shape, dt=F32):
        self.uid += 1
        t = st.enter_context(self.nc.psum_tensor("%s_%d" % (name, self.uid), list(shape), dt))
        return Tk(t, name)

    def dma_chan(self, name):
        for c in self.chans:
            if c.name == name:
                return c
        sem = self.stack.enter_context(self.nc.semaphore("d_" + name))
        c = Chan(sem, name)
        self.chans.append(c)
        return c

    def _need(self, en, reads, writes):
        need = {}

        def add(dep):
            if dep is None:
                return
            ch, tk = dep
            if need.get(ch, 0) < tk:
                need[ch] = tk

        my = self.engs[en][1]
        for r in reads:
            add(r.w)
        for w in writes:
            add(w.w)
            for d in w.rs:
                if d[0] is my:
                    continue
                add(d)
        return need

    def _waits(self, en, need, skip_self_pe):
        eng, ch = self.engs[en]
        wd = self.waited[en]
        for c, tk in need.items():
            if skip_self_pe and c is ch:
                continue
            if wd.get(c, 0) >= tk:
                continue
            eng.wait_ge(c.sem, tk)
            wd[c] = tk

    def _record(self, dep, reads, writes):
        for w in writes:
            w.w = dep
            w.rs = []
        for r in reads:
            r.rs.append(dep)
            if len(r.rs) > 16:
                m = {}
                for c, t in r.rs:
                    if m.get(c, 0) < t:
                        m[c] = t
                r.rs = list(m.items())

    def op(self, en, fn, reads=(), writes=()):
        eng, ch = self.engs[en]
        self._waits(en, self._need(en, reads, writes), en == "pe")
        ins = fn(eng)
        ch.count += 1
        ins.then_inc(ch.sem, 1)
        self._record((ch, ch.count), reads, writes)
        return ins

    def dma(self, chan, out_ap, in_ap, reads=(), writes=(), en="sp"):
        eng, _ = self.engs[en]
        self._waits(en, self._need(en, reads, writes), False)
        ins = eng.dma_start(out=out_ap, in_=in_ap)
        chan.count += 16
        ins.then_inc(chan.sem, 16)
        self._record((chan, chan.count), reads, writes)
        return ins

    def dma_st(self, src_tk, out_ap, in_ap):
        return self.dma(self.dma_chan("o_" + src_tk.name), out_ap, in_ap, reads=[src_tk])

    def barrier(self):
        chs = [c for (_, c) in self.engs.values()] + self.chans
        for en, (eng, mych) in self.engs.items():
            wd = self.waited[en]
            for c in chs:
                if c is mych or c.count == 0:
                    continue
                if wd.get(c, 0) >= c.count:
                    continue
                eng.wait_ge(c.sem, c.count)
                wd[c] = c.count


class Stream:
    def __init__(self, fw, st, nbuf=6, look=4):
        self.fw = fw
        self.bufs = [fw.sb(st, "wbuf%d" % i, [P, 4096], BF16) for i in range(nbuf)]
        self.chs = [fw.dma_chan("wb%d" % i) for i in range(nbuf)]
        self.look = look
        self.plan = []
        self.issued = 0
        self.taken = 0

    def add(self, dram_ap, shape, parts=None):
        self.plan.append((dram_ap, shape, parts))

    def _issue(self, i):
        ap, shape, parts = self.plan[i]
        b = i % len(self.bufs)
        dst = self.view(self.bufs[b], shape)
        if parts is None:
            self.fw.dma(self.chs[b], dst, ap, writes=[self.bufs[b]])
        else:
            for src_ap, sel in parts:
                self.fw.dma(self.chs[b], sel(dst), src_ap, writes=[self.bufs[b]])

    @staticmethod
    def view(buf, shape):
        n = int(np.prod(shape[1:]))
        v = buf.t[:, 0:n]
        if len(shape) == 3:
            v = v.rearrange("p (a b) -> p a b", a=shape[1])
        elif len(shape) == 4:
            v = v.rearrange("p (a b c) -> p a b c", a=shape[1], b=shape[2])
        return v

    def next(self):
        i = self.taken
        while self.issued < min(len(self.plan), i + 1 + self.look):
            self._issue(self.issued)
            self.issued += 1
        self.taken += 1
        b = i % len(self.bufs)
        return self.bufs[b], self.view(self.bufs[b], self.plan[i][1])


def build(nc, n_seq, S, depth, dbg=False):
    T = n_seq * S
    NG = T // G
    NB = T // P
    NBS = S // P
    GPS = S // G

    def din(name, shape):
        return nc.dram_tensor(name, list(shape), F32, kind="ExternalInput").ap()

    def dscr(name, shape, dt):
        return nc.dram_tensor(name, list(shape), dt).ap()

    xT = din("xT", [KC, P, T])
    outT = nc.dram_tensor("outT", [KC, P, T], F32, kind="ExternalOutput").ap()
    consts_d = din("consts", [P, NCONST])
    pvec_d = din("pvec", [depth, P, NPV])
    w_ffn_in = {1: din("ffn1_w_in", [depth, D, 2 * DFF]), 2: din("ffn2_w_in", [depth, D, 2 * DFF])}
    w_ffn_out = {1: din("ffn1_w_out", [depth, DFF, D]), 2: din("ffn2_w_out", [depth, DFF, D])}
    w_in = din("w_in", [depth, D, NIN])
    w_out = din("w_out", [depth, D, D])

    wb_ffn_in = {f: dscr("wb_ffn%d_in" % f, [depth, D, 2 * DFF], BF16) for f in (1, 2)}
    wb_ffn_out = {f: dscr("wb_ffn%d_out" % f, [depth, DFF, D], BF16) for f in (1, 2)}
    wb_in = dscr("wb_in", [depth, D, NIN], BF16)
    wb_out = dscr("wb_out", [depth, D, D], BF16)
    XR = dscr("XR", [KC, P, T], F32)
    QF = dscr("QF", [n_seq, 8, 70, S], BF16)
    KF = dscr("KF", [n_seq, 8, 70, S], BF16)
    VF = dscr("VF", [P, NB, 520], BF16)
    GQ = dscr("GQ", [4, P, T], BF16)
    GK = dscr("GK", [4, P, T], BF16)
    GV = dscr("GV", [4, P, T], BF16)
    GGs = dscr("GGs", [4, P, T], BF16)
    GS = dscr("GS", [P, NB, 8], F32)
    Y = dscr("Y", [KC, P, T], BF16)

    with contextlib.ExitStack() as top:
        fw = FW(nc, top)
        top.enter_context(nc.Block())
        op, dma = fw.op, fw.dma

        cf = fw.sb(top, "cf", [P, C_FM], F32)
        cb = fw.sb(top, "cb", [P, NCONST], BF16)
        ch_c = fw.dma_chan("c")
        dma(ch_c, cf[:], consts_d[:, 0:C_FM], writes=[cf])
        op("dve", lambda e: e.tensor_copy(cb[:, 0:C_FM], cf[:]), reads=[cf], writes=[cb])
        with contextlib.ExitStack() as tst:
            ctmp = fw.sb(tst, "ctmp", [P, 2048], F32)
            dma(fw.dma_chan("c2"), ctmp[:], consts_d[:, C_FM:NCONST], writes=[ctmp])
            op("dve", lambda e: e.tensor_copy(cb[:, C_FM:NCONST], ctmp[:]), reads=[ctmp], writes=[cb])
            fw.barrier()
        negones = fw.sb(top, "negones", [P, 128], F32)
        op("pool", lambda e: e.memset(negones[:], -1.0), writes=[negones])
        epsc = fw.sb(top, "epsc", [P, 1], F32)
        op("pool", lambda e: e.memset(epsc[:], EPS), writes=[epsc])
        onec = fw.sb(top, "onec", [P, 1], F32)
        op("pool", lambda e: e.memset(onec[:], 1.0), writes=[onec])
        ones_bf = cb.t[:, C_ONE:C_ONE + 128]
        ident_bf = cb.t[:, C_ID:C_ID + 128]

        ch_p = fw.dma_chan("p")
        prm = {}
        pv_tiles = []
        for l in range(depth):
            pv_t = fw.sb(top, "pvec%d" % l, [P, NPV], F32)
            dma(ch_p, pv_t[:], pvec_d[l], writes=[pv_t])
            pv_tiles.append(pv_t)
        fw.barrier()
        for l in range(depth):
            pv_t = pv_tiles[l]
            d = {}

            class _V:
                def __init__(self, tk, c0, shape=None):
                    self.tk, self.c0, self.shape = tk, c0, shape
            gq = fw.sb(top, "gq%d" % l, [P, 1], F32)
            op("dve", lambda e: e.tensor_scalar(gq[:], pv_t[:, PV_GQ:PV_GQ + 1], 0.125, None, ALU.mult), reads=[pv_t], writes=[gq])
            gk = fw.sb(top, "gk%d" % l, [P, 1], F32)
            op("dve", lambda e: e.tensor_copy(gk[:], pv_t[:, PV_GK:PV_GK + 1]), reads=[pv_t], writes=[gk])
            nfb = fw.sb(top, "nfb%d" % l, [8, 1], F32)
            op("dve", lambda e: e.tensor_scalar(nfb[:], pv_t[0:8, PV_FB:PV_FB + 1], -1.0, None, ALU.mult), reads=[pv_t], writes=[nfb])
            dtb = fw.sb(top, "dtb%d" % l, [P, 4, 4], F32)
            nA = fw.sb(top, "nA%d" % l, [P, 4, 4], F32)
            eA = fw.sb(top, "eA%d" % l, [P, 4], F32)
            op("act", lambda e: e.activation(out=eA[:], in_=pv_t[:, PV_AL:PV_AL + 4], func=AF.Exp), reads=[pv_t], writes=[eA])
            for tb in range(4):
                op("dve", lambda e, tb=tb: e.tensor_copy(dtb[:, tb, :], pv_t[:, PV_DT:PV_DT + 4]), reads=[pv_t], writes=[dtb])
                op("dve", lambda e, tb=tb: e.tensor_scalar(nA[:, tb, :], eA[:], -1.0, None, ALU.mult), reads=[eA], writes=[nA])
            n1 = fw.sb(top, "n1_%d" % l, [P, KC], F32)
            n2 = fw.sb(top, "n2_%d" % l, [P, KC], F32)
            nm = fw.sb(top, "nm_%d" % l, [P, KC], F32)
            cw = fw.sb(top, "cw%d" % l, [P, 12, 4], F32)
            gon = fw.sb(top, "gon%d" % l, [P, 1], F32)
            op("dve", lambda e: e.tensor_copy(n1[:], pv_t[:, PV_N1:PV_N1 + 8]), reads=[pv_t], writes=[n1])
            op("dve", lambda e: e.tensor_copy(n2[:], pv_t[:, PV_N2:PV_N2 + 8]), reads=[pv_t], writes=[n2])
            op("dve", lambda e: e.tensor_copy(nm[:], pv_t[:, PV_NM:PV_NM + 8]), reads=[pv_t], writes=[nm])
            op("dve", lambda e: e.tensor_copy(cw[:].rearrange("p c k -> p (c k)"), pv_t[:, PV_CW:PV_CW + 48]), reads=[pv_t], writes=[cw])
            op("dve", lambda e: e.tensor_copy(gon[:], pv_t[:, PV_GON:PV_GON + 1]), reads=[pv_t], writes=[gon])
            d.update(n1=n1, n2=n2, nm=nm, gq=gq, gk=gk, nfb=nfb, cw=cw, dtb=dtb, nA=nA, gon=gon)
            prm[l] = d

        ch_w = fw.dma_chan("w")
        for l in range(depth):
            for f in (1, 2):
                for r0 in range(0, D, 256):
                    dma(ch_w, wb_ffn_in[f][l, r0:r0 + 256, :], w_ffn_in[f][l, r0:r0 + 256, :], en="pool")
                for r0 in range(0, DFF, 704):
                    dma(ch_w, wb_ffn_out[f][l, r0:r0 + 704, :], w_ffn_out[f][l, r0:r0 + 704, :], en="pool")
            for r0 in range(0, D, 512):
                dma(ch_w, wb_in[l, r0:r0 + 512, :], w_in[l, r0:r0 + 512, :], en="pool")
                dma(ch_w, wb_out[l, r0:r0 + 512, :], w_out[l, r0:r0 + 512, :], en="pool")
        with contextlib.ExitStack() as tst:
            onesrow = fw.sb(tst, "onesrow", [8, 3, G], BF16)
            op("pool", lambda e: e.memset(onesrow[:], 1.0), writes=[onesrow])
            for sq_ in range(n_seq):
                for tq in range(0, S, G):
                    dma(ch_w, QF[sq_, :, 67:70, tq:tq + G], onesrow[:], reads=[onesrow])
                    dma(ch_w, KF[sq_, :, 64:67, tq:tq + G], onesrow[:], reads=[onesrow])
            fw.barrier()

        def group_pass(src, l_mix, ffn_a, ffn_b, l_proj, final):
            with contextlib.ExitStack() as st:
                X = [fw.sb(st, "X%d" % i, [P, KC, G], F32) for i in range(2)]
                chX = [fw.dma_chan("x%d" % i) for i in range(2)]
                chY = fw.dma_chan("y")
                xn = fw.sb(st, "xn", [P, KC, G], BF16)
                Yb = xn
                H = fw.sb(st, "H", [P, NJ, G], BF16)
                sq = H
                msb = [fw.sb(st, "ms%d" % i, [P, G], F32) for i in range(2)]
                rsb = [fw.sb(st, "rstd%d" % i, [P, G], F32) for i in range(2)]
                rs_i = [0]
                sgt = [fw.sb(st, "sgt%d" % i, [P, G], F32) for i in range(2)]
                bank = [fw.ps(st, "bk%d" % i, [P, G], F32) for i in range(8)]
                bg, bu, bo, bm = bank[0:2], bank[2:4], bank[4:6], bank[6:8]
                chO = fw.dma_chan("o")
                stream = Stream(fw, st, nbuf=5, look=3)
                if l_proj is not None:
                    wsm = fw.sb(st, "wsm", [P, KC, 16], BF16)
                    chS = fw.dma_chan("s")
                    wv = wb_in[l_proj].rearrange("(kc p) c -> p kc c", p=P)
                    dma(chS, wsm[:, :, 0:8], wv[:, :, O_FF:O_FF + 8], writes=[wsm])
                    dma(chS, wsm[:, :, 8:16], wv[:, :, O_GA:O_GA + 8], writes=[wsm])
                    cbuf = [fw.sb(st, "cbuf%d" % i, [P, G + 3], F32) for i in range(12)]
                    acc = [fw.sb(st, "acc%d" % i, [P, G], F32) for i in range(2)]
                    sqb = [fw.sb(st, "sqb%d" % i, [P, G], BF16) for i in range(2)]
                    ob = [fw.sb(st, "ob%d" % i, [P, G], BF16) for i in range(4)]
                    vt = [fw.sb(st, "vt%d" % i, [P, 4, 8, 65], BF16) for i in range(2)]
                    for v in vt:
                        op("pool", lambda e, v=v: e.memset(v[:], 1.0), writes=[v])
                    sm_e = fw.sb(st, "sm_e", [8, G], F32)
                    sm_l = sm_e
                    cum = fw.sb(st, "cum", [8, G], F32)
                    carry = fw.sb(st, "carry", [8, 1], F32)
                    r32 = fw.sb(st, "r32", [8, G], F32)
                    t32 = fw.sb(st, "t32", [8, G], F32)
                    ex = [fw.sb(st, "ex0", [8, 6, G], BF16)] * 2
                    onesr = fw.sb(st, "onesr", [8, G], F32)
                    op("pool", lambda e: e.memset(onesr[:], 1.0), writes=[onesr])
                    tz = fw.sb(st, "tz", [P, 4, 8], F32)
                    te = fw.sb(st, "te", [P, 4, 8], F32)
                    gs = [fw.sb(st, "gs%d" % i, [P, 4, 8], F32) for i in range(2)]

                def plan_ffn(f, l):
                    wi = wb_ffn_in[f][l].rearrange("(kc p) (gu c) -> p kc gu c", p=P, gu=2)
                    for jb in range(NJ // 2):
                        stream.add(None, [P, KC, 2, 256], parts=[
                            (wi[:, :, 0, jb * 256:(jb + 1) * 256], lambda v: v[:, :, 0, :]),
                            (wi[:, :, 1, jb * 256:(jb + 1) * 256], lambda v: v[:, :, 1, :])])
                    wo = wb_ffn_out[f][l].rearrange("(j p) c -> p j c", p=P)
                    for m in range(KC):
                        stream.add(wo[:, :, m * 128:(m + 1) * 128], [P, NJ, 128])

                for g in range(NG):
                    if l_mix is not None:
                        wm = wb_out[l_mix].rearrange("(kc p) c -> p kc c", p=P)
                        for hf in range(2):
                            stream.add(wm[:, :, hf * 512:(hf + 1) * 512], [P, KC, 512])
                    if ffn_a is not None:
                        plan_ffn(*ffn_a)
                    if ffn_b is not None:
                        plan_ffn(*ffn_b)
                    if l_proj is not None:
                        wv = wb_in[l_proj].rearrange("(kc p) c -> p kc c", p=P)
                        for o in (O_GQ, O_GK, O_GV, O_GG, O_FQ, O_FK, O_FV):
                            stream.add(wv[:, :, o:o + 512], [P, KC, 512])

                def mm(out_tk, out_ap, l_tk, l_ap, r_tk, r_ap, start, stop):
                    op("pe", lambda e: e.matmul(out_ap, l_ap, r_ap, start=start, stop=stop),
                       reads=[l_tk, r_tk], writes=[out_tk])

                def rsqrt_from(ps_tk, ps_ap, scale):
                    ms, rstd = msb[rs_i[0] % 2], rsb[rs_i[0] % 2]
                    rs_i[0] += 1
                    op("act", lambda e: e.activation(out=ms[:], in_=ps_ap, func=AF.Ln, bias=epsc[:], scale=scale),
                       reads=[ps_tk, epsc], writes=[ms])
                    op("act", lambda e: e.activation(out=rstd[:], in_=ms[:], func=AF.Exp, scale=-0.5),
                       reads=[ms], writes=[rstd])
                    return rstd

                def rmsnorm(Xt, wvec):
                    op("act", lambda e: e.activation(out=sq[:, 0:KC, :], in_=Xt[:], func=AF.Square), reads=[Xt], writes=[sq])
                    b = bm[0]
                    for kc in range(KC):
                        mm(b, b[:], cb, ones_bf, sq, sq[:, kc, :], kc == 0, kc == KC - 1)
                    rstd = rsqrt_from(b, b[:], 1.0 / D)
                    for kc in range(KC):
                        op("dve", lambda e, kc=kc: e.scalar_tensor_tensor(
                            out=xn[:, kc, :], in0=Xt[:, kc, :], scalar=wvec[:, kc:kc + 1], in1=rstd[:],
                            op0=ALU.mult, op1=ALU.mult), reads=[Xt, wvec, rstd], writes=[xn])

                def ffn(Xt, f, l):
                    rmsnorm(Xt, prm[l]["n%d" % f])
                    for jb in range(NJ // 2):
                        wtk, wp = stream.next()
                        for jj in range(2):
                            j = jb * 2 + jj
                            pg, pu = bg[j % 2], bu[j % 2]
                            for kc in range(KC):
                                mm(pg, pg[:], wtk, wp[:, kc, 0, jj * 128:(jj + 1) * 128], xn, xn[:, kc, :], kc == 0, kc == KC - 1)
                            for kc in range(KC):
                                mm(pu, pu[:], wtk, wp[:, kc, 1, jj * 128:(jj + 1) * 128], xn, xn[:, kc, :], kc == 0, kc == KC - 1)
                            sg = sgt[j % 2]
                            op("act", lambda e, pg=pg, sg=sg: e.activation(out=sg[:], in_=pg[:], func=AF.Silu),
                               reads=[pg], writes=[sg])
                            op("dve", lambda e, pu=pu, sg=sg, j=j: e.tensor_tensor(H[:, j, :], pu[:], sg[:], ALU.mult),
                               reads=[pu, sg], writes=[H])
                    for m in range(KC):
                        wtk, wp = stream.next()
                        po = bo[m % 2]
                        for j in range(NJ):
                            mm(po, po[:], wtk, wp[:, j, :], H, H[:, j, :], j == 0, j == NJ - 1)
                        op("dve", lambda e, po=po, m=m: e.scalar_tensor_tensor(
                            out=Xt[:, m, :], in0=po[:], scalar=0.5, in1=Xt[:, m, :], op0=ALU.mult, op1=ALU.add),
                            reads=[po, Xt], writes=[Xt])

                dma(chX[0], X[0][:], src[:, :, 0:G].rearrange("kc p t -> p kc t"), writes=[X[0]])
                for g in range(NG):
                    Xt = X[g % 2]
                    t0 = g * G
                    seq, ts0 = t0 // S, t0 % S
                    if g + 1 < NG:
                        dma(chX[(g + 1) % 2], X[(g + 1) % 2][:],
                            src[:, :, t0 + G:t0 + 2 * G].rearrange("kc p t -> p kc t"), writes=[X[(g + 1) % 2]])
                    if l_mix is not None:
                        dma(chY, Yb[:], Y[:, :, t0:t0 + G].rearrange("kc p t -> p kc t"), writes=[Yb])
                        for hf in range(2):
                            wtk, wp = stream.next()
                            for mm_ in range(4):
                                m = hf * 4 + mm_
                                po = bo[m % 2]
                                for kc in range(KC):
                                    mm(po, po[:], wtk, wp[:, kc, mm_ * 128:(mm_ + 1) * 128], Yb, Yb[:, kc, :], kc == 0, kc == KC - 1)
                                op("dve", lambda e, po=po, m=m: e.tensor_tensor(Xt[:, m, :], po[:], Xt[:, m, :], ALU.add),
                                   reads=[po, Xt], writes=[Xt])
                    if ffn_a is not None:
                        ffn(Xt, *ffn_a)
                    if ffn_b is not None:
                        ffn(Xt, *ffn_b)
                    if final:
                        fw.dma_st(Xt, outT[:, :, t0:t0 + G].rearrange("kc p t -> p kc t"), Xt[:])
                        continue
                    l = l_proj
                    pr = prm[l]
                    fw.dma_st(Xt, XR[:, :, t0:t0 + G].rearrange("kc p t -> p kc t"), Xt[:])
                    rmsnorm(Xt, pr["nm"])
                    blk0 = t0 // P
                    if ts0 == 0:
                        for ci in range(12):
                            op("pool", lambda e, ci=ci: e.memset(cbuf[ci][:, 0:3], 0.0), writes=[cbuf[ci]])
                    for gi, (dst, kind) in enumerate(((GQ, "q"), (GK, "k"), (GV, "v"), (GGs, "g"))):
                        wtk, wp = stream.next()
                        for c in range(4):
                            pb = bg[c % 2]
                            for kc in range(KC):
                                mm(pb, pb[:], wtk, wp[:, kc, c * 128:(c + 1) * 128], xn, xn[:, kc, :], kc == 0, kc == KC - 1)
                            o_t = ob[c % 4]
                            if kind == "g":
                                op("act", lambda e, pb=pb, o_t=o_t: e.activation(out=o_t[:], in_=pb[:], func=AF.Silu),
                                   reads=[pb], writes=[o_t])
                                fw.dma_st(o_t, dst[c, :, t0:t0 + G], o_t[:])
                                continue
                            ci = gi * 4 + c
                            cbt = cbuf[ci]
                            a = acc[c % 2]
                            op("act", lambda e, pb=pb, cbt=cbt: e.activation(out=cbt[:, 3:3 + G], in_=pb[:], func=AF.Copy),
                               reads=[pb], writes=[cbt])
                            op("dve", lambda e, a=a, ci=ci, cbt=cbt: e.tensor_scalar(a[:], cbt[:, 3:3 + G], pr["cw"][:, ci, 3:4], None, ALU.mult),
                               reads=[cbt, pr["cw"]], writes=[a])
                            for k in (2, 1, 0):
                                op("dve", lambda e, a=a, ci=ci, k=k, cbt=cbt: e.scalar_tensor_tensor(
                                    out=a[:], in0=cbt[:, k:k + G], scalar=pr["cw"][:, ci, k:k + 1], in1=a[:],
                                    op0=ALU.mult, op1=ALU.add), reads=[cbt, pr["cw"], a], writes=[a])
                            op("pool", lambda e, cbt=cbt: e.tensor_copy(cbt[:, 0:3], cbt[:, G:G + 3]),
                               reads=[cbt], writes=[cbt])
                            if kind == "v":
                                op("act", lambda e, a=a, o_t=o_t: e.activation(out=o_t[:], in_=a[:], func=AF.Silu),
                                   reads=[a], writes=[o_t])
                                fw.dma_st(o_t, dst[c, :, t0:t0 + G], o_t[:])
                            else:
                                op("act", lambda e, a=a, cbt=cbt: e.activation(out=cbt[:, 3:3 + G], in_=a[:], func=AF.Silu),
                                   reads=[a], writes=[cbt])
                    for gi, (dst, kind) in enumerate(((GQ, "q"), (GK, "k"))):
                        for c in range(4):
                            ci = gi * 4 + c
                            cbt = cbuf[ci]
                            sb_ = sqb[c % 2]
                            o_t = ob[c % 4]
                            op("act", lambda e, cbt=cbt, sb_=sb_: e.activation(out=sb_[:], in_=cbt[:, 3:3 + G], func=AF.Square),
                               reads=[cbt], writes=[sb_])
                            pn = bm[c % 2]
                            mm(pn, pn[:], cb, ones_bf, sb_, sb_[:], True, True)
                            rstd = rsqrt_from(pn, pn[:], 1.0)
                            qs = (128.0 ** -0.5) if kind == "q" else 1.0
                            op("dve", lambda e, cbt=cbt, o_t=o_t, qs=qs, rstd=rstd: e.scalar_tensor_tensor(
                                out=o_t[:], in0=cbt[:, 3:3 + G], scalar=qs, in1=rstd[:], op0=ALU.mult, op1=ALU.mult),
                                reads=[cbt, rstd], writes=[o_t])
                            fw.dma_st(o_t, dst[c, :, t0:t0 + G], o_t[:])
                    for dst, gvec in ((QF, pr["gq"]), (KF, pr["gk"])):
                        wtk, wp = stream.next()
                        for c in range(4):
                            pb = bg[c % 2]
                            for kc in range(KC):
                                mm(pb, pb[:], wtk, wp[:, kc, c * 128:(c + 1) * 128], xn, xn[:, kc, :], kc == 0, kc == KC - 1)
                            sb_ = sqb[c % 2]
                            op("act", lambda e, pb=pb, sb_=sb_: e.activation(out=sb_[:], in_=pb[:], func=AF.Square), reads=[pb], writes=[sb_])
                            pn = bm[c % 2]
                            mm(pn, pn[:], cb, cb.t[:, C_BD:C_BD + 128], sb_, sb_[:], True, True)
                            rstd = rsqrt_from(pn, pn[:], 1.0 / 64)
                            o_t = ob[c % 4]
                            op("dve", lambda e, pb=pb, o_t=o_t, gvec=gvec, rstd=rstd: e.scalar_tensor_tensor(
                                out=o_t[:], in0=pb[:], scalar=gvec[:, 0:1], in1=rstd[:], op0=ALU.mult, op1=ALU.mult),
                                reads=[pb, gvec, rstd], writes=[o_t])
                            for hh in range(2):
                                fw.dma_st(o_t, dst[seq, 2 * c + hh, 0:64, ts0:ts0 + G], o_t[hh * 64:(hh + 1) * 64, :])
                    wtk, wp = stream.next()
                    vtile = vt[g % 2]
                    for tb in range(4):
                        pb = bu[tb % 2]
                        for kc in range(KC):
                            mm(pb, pb[:], xn, xn[:, kc, tb * 128:(tb + 1) * 128], wtk, wp[:, kc, :], kc == 0, kc == KC - 1)
                        op("act", lambda e, pb=pb, tb=tb: e.activation(
                            out=vtile[:, tb, :, 0:64], in_=pb[:].rearrange("p (h d) -> p h d", h=8), func=AF.Copy),
                            reads=[pb], writes=[vtile])
                    fw.dma_st(vtile, VF[:, blk0:blk0 + 4, :], vtile[:].rearrange("p a h d -> p a (h d)"))
                    pb = bm[1]
                    for kc in range(KC):
                        mm(pb, pb[0:16, :], wsm, wsm[:, kc, :], xn, xn[:, kc, :], kc == 0, kc == KC - 1)
                    op("act", lambda e: e.activation(out=sm_e[:], in_=pb[0:8, :], func=AF.Exp, bias=pr["nfb"][:], scale=-1.0),
                       reads=[pb, pr["nfb"]], writes=[sm_e])
                    op("act", lambda e: e.activation(out=sm_l[:], in_=sm_e[:], func=AF.Ln, bias=onec[0:8, :], scale=1.0),
                       reads=[sm_e, onec], writes=[sm_l])
                    if ts0 == 0:
                        op("dve", lambda e: e.memset(carry[:], 0.0), writes=[carry])
                    op("dve", lambda e: e.tensor_tensor_scan(cum[:], onesr[:], sm_l[:], carry[:], ALU.mult, ALU.subtract),
                       reads=[onesr, sm_l, carry], writes=[cum])
                    op("dve", lambda e: e.tensor_copy(carry[:], cum[:, G - 1:G]), reads=[cum], writes=[carry])
                    ext = ex[g % 2]
                    op("dve", lambda e: e.tensor_copy(ext[:, 0, :], cum[:]), reads=[cum], writes=[ext])
                    op("dve", lambda e: e.tensor_copy(t32[:], ext[:, 0, :]), reads=[ext], writes=[t32])
                    op("dve", lambda e: e.tensor_tensor(r32[:], cum[:], t32[:], ALU.subtract), reads=[cum, t32], writes=[r32])
                    op("dve", lambda e: e.tensor_copy(ext[:, 1, :], r32[:]), reads=[r32], writes=[ext])
                    op("dve", lambda e: e.tensor_copy(t32[:], ext[:, 1, :]), reads=[ext], writes=[t32])
                    op("dve", lambda e: e.tensor_tensor(r32[:], r32[:], t32[:], ALU.subtract), reads=[r32, t32], writes=[r32])
                    op("dve", lambda e: e.tensor_copy(ext[:, 2, :], r32[:]), reads=[r32], writes=[ext])
                    op("dve", lambda e: e.tensor_scalar(ext[:, 3:6, :], ext[:, 0:3, :], -1.0, None, ALU.mult), reads=[ext], writes=[ext])
                    fw.dma_st(ext, QF[seq, :, 64:67, ts0:ts0 + G], ext[:, 0:3, :])
                    fw.dma_st(ext, KF[seq, :, 67:70, ts0:ts0 + G], ext[:, 3:6, :])
                    pb = bm[0]
                    for tb in range(4):
                        for kc in range(KC):
                            mm(pb, pb[:, tb * 16:tb * 16 + 16], xn, xn[:, kc, tb * 128:(tb + 1) * 128], wsm, wsm[:, kc, :], kc == 0, kc == KC - 1)
                    pv = pb[:, 0:64].rearrange("p (a c) -> p a c", a=4)
                    gst = gs[g % 2]
                    op("dve", lambda e: e.tensor_tensor(tz[:, :, 0:4], pv[:, :, 8:12], pr["dtb"][:], ALU.add), reads=[pb, pr["dtb"]], writes=[tz])
                    op("dve", lambda e: e.tensor_scalar(tz[:, :, 4:8], pv[:, :, 12:16], -1.0, None, ALU.mult), reads=[pb], writes=[tz])
                    op("act", lambda e: e.activation(out=te[:], in_=tz[:], func=AF.Exp), reads=[tz], writes=[te])
                    op("act", lambda e: e.activation(out=tz[:, :, 0:4], in_=te[:, :, 0:4], func=AF.Ln, bias=onec[:], scale=1.0),
                       reads=[te, onec], writes=[tz])
                    op("dve", lambda e: e.tensor_tensor(gst[:, :, 0:4], tz[:, :, 0:4], pr["nA"][:], ALU.mult), reads=[tz, pr["nA"]], writes=[gst])
                    op("dve", lambda e: e.tensor_scalar(te[:, :, 4:8], te[:, :, 4:8], 1.0, None, ALU.add), reads=[te], writes=[te])
                    op("dve", lambda e: e.reciprocal(gst[:, :, 4:8], te[:, :, 4:8]), reads=[te], writes=[gst])
                    fw.dma_st(gst, GS[:, blk0:blk0 + 4, :], gst[:])
            fw.barrier()

        def fox_phase():
            with contextlib.ExitStack() as st:
                Kt = [fw.sb(st, "Kt%d" % i, [P, S], BF16) for i in range(2)]
                Qt = [fw.sb(st, "Qt%d" % i, [P, S], BF16) for i in range(2)]
                chK = [fw.dma_chan("k%d" % i) for i in range(2)]
                chQ = [fw.dma_chan("q%d" % i) for i in range(2)]
                Vt = fw.sb(st, "Vt", [P, NBS, 520], BF16)
                chV = fw.dma_chan("v")
                pT = [fw.sb(st, "pT%d" % i, [P, G], BF16) for i in range(3)]
                bS = [fw.ps(st, "bS%d" % i, [P, G], F32) for i in range(3)]
                bO = [fw.ps(st, "bO%d" % i, [P, G], F32) for i in range(2)]
                bB = fw.ps(st, "bB", [P, G], F32)
                rc = fw.sb(st, "rc", [P, G], F32)
                bc = fw.sb(st, "bc", [64, G], F32)
                yo = [fw.sb(st, "yo%d" % i, [64, S], BF16) for i in range(2)]
                chO = fw.dma_chan("fo")
                fm = cb.t[:, C_FM:C_FM + 2048].rearrange("p (r t) -> p r t", r=4)
                Vv = Vt.t[:].rearrange("p a (h d) -> p a h d", h=8)
                blocks = []
                for seq in range(n_seq):
                    for h in range(8):
                        for i in range(S // G):
                            for j in range(4 * (i + 1)):
                                blocks.append((seq, h, i, j))
                hctx = {}

                def head_ctx(seq, h):
                    if (seq, h) not in hctx:
                        hi = len(hctx)
                        K_, Q_ = Kt[hi % 2], Qt[hi % 2]
                        dma(chK[hi % 2], K_[0:70, :], KF[seq, h], writes=[K_])
                        dma(chQ[hi % 2], Q_[0:70, :], QF[seq, h], writes=[Q_])
                        hctx[(seq, h)] = (K_, Q_, yo[hi % 2])
                    return hctx[(seq, h)]

                def issue_qk(n):
                    seq, h, i, j = blocks[n]
                    K_, Q_, _ = head_ctx(seq, h)
                    ps_ = bS[n % 3]
                    diag = j >= 4 * i
                    op("pe", lambda e: e.matmul(ps_[:], K_[0:70, j * 128:(j + 1) * 128], Q_[0:70, i * G:(i + 1) * G],
                                                start=True, stop=not diag), reads=[K_, Q_], writes=[ps_])
                    if diag:
                        r = j - 4 * i
                        op("pe", lambda e: e.matmul(ps_[:], ident_bf, fm[:, r, :], start=False, stop=True),
                           reads=[cb], writes=[ps_])

                def issue_rest(n):
                    seq, h, i, j = blocks[n]
                    _, _, yt = head_ctx(seq, h)
                    ps_, pt = bS[n % 3], pT[n % 3]
                    po = bO[i % 2]
                    nj = 4 * (i + 1)
                    if h == 0 and i == 0 and j == 0:
                        dma(chV, Vt[:], VF[:, seq * NBS:(seq + 1) * NBS, :], writes=[Vt])
                    op("act", lambda e: e.activation(out=pt[:], in_=ps_[:], func=AF.Exp), reads=[ps_], writes=[pt])
                    op("pe", lambda e: e.matmul(po[0:65, :], Vv[:, j, h, :], pt[:], start=(j == 0), stop=(j == nj - 1)),
                       reads=[Vt, pt], writes=[po])
                    if j == nj - 1:
                        op("dve", lambda e: e.reciprocal(rc[64:65, :], po[64:65, :]), reads=[po], writes=[rc])
                        op("pe", lambda e: e.matmul(bB[0:64, :], cf[64:65, C_ONE:C_ONE + 64], rc[64:65, :], start=True, stop=True),
                           reads=[cf, rc], writes=[bB])
                        op("act", lambda e: e.activation(out=bc[:], in_=bB[0:64, :], func=AF.Copy), reads=[bB], writes=[bc])
                        op("dve", lambda e: e.tensor_tensor(yt[:, i * G:(i + 1) * G], po[0:64, :], bc[:], ALU.mult),
                           reads=[po, bc], writes=[yt])
                        if i == S // G - 1:
                            fw.dma_st(yt, Y[h // 2, (h % 2) * 64:(h % 2) * 64 + 64, seq * S:(seq + 1) * S], yt[:])

                LA = int(os.environ.get("K_LA", "2"))
                for n in range(len(blocks) + LA):
                    if n < len(blocks):
                        issue_qk(n)
                    if n >= LA:
                        issue_rest(n - LA)
            fw.barrier()

        def gdn_phase(l):
            pr = prm[l]
            with contextlib.ExitStack() as st:
                def sbt(name, shape, dt):
                    return fw.sb(st, name, shape, dt)
                qkv = [[sbt("%s%d" % (nm, i), [P, 4, G], BF16) for nm in ("gq", "gk", "gv", "gg")] for i in range(2)]
                chq = [fw.dma_chan("g%d" % i) for i in range(2)]
                gsb = sbt("gsb", [P, NBS, 8], F32)
                chg = fw.dma_chan("gs")
                U4 = sbt("U4", [P, 4, 128], F32)
                id4 = sbt("id4", [P, 4, 128], BF16)
                nmt4 = sbt("nmt4", [P, 4, 128], BF16)
                m014 = sbt("m014", [P, 4, 128], BF16)
                for h in range(4):
                    op("dve", lambda e, h=h: e.tensor_copy(U4[:, h, :], cf[:, C_U:C_U + 128]), reads=[cf], writes=[U4])
                    op("dve", lambda e, h=h: e.tensor_copy(id4[:, h, :], cb[:, C_ID:C_ID + 128]), reads=[cb], writes=[id4])
                    op("dve", lambda e, h=h: e.tensor_copy(nmt4[:, h, :], cb[:, C_NMT:C_NMT + 128]), reads=[cb], writes=[nmt4])
                    op("dve", lambda e, h=h: e.tensor_copy(m014[:, h, :], cb[:, C_M01:C_M01 + 128]), reads=[cb], writes=[m014])
                ng = sbt("ng", [P, 4], F32)
                nbeta = sbt("nbeta", [P, 4], F32)
                NG1 = sbt("NG1", [P, 4, 128], F32)
                esm = sbt("esm", [P, 12], F32)
                E = sbt("E", [P, 4, 128], BF16)
                qdT = sbt("qdT", [P, 4, 128], BF16)
                DT = sbt("DT", [P, 4, 128], BF16)
                DTs = sbt("DTs", [P, 4, 128], BF16)
                attnT = sbt("attnT", [P, 4, 128], BF16)
                Zc = [sbt("Zc%d" % i, [P, 4, 128], BF16) for i in range(2)]
                ZAc = [sbt("ZAc%d" % i, [P, 4, 128], BF16) for i in range(2)]
                Pc = [sbt("Pc%d" % i, [P, 4, 128], BF16) for i in range(2)]
                kg = sbt("kg", [P, 4, 128], BF16)
                kd = sbt("kd", [P, 4, 128], BF16)
                vtk = sbt("vtk", [P, 4, 128], BF16)
                ktk = sbt("ktk", [P, 4, 128], BF16)
                nwT = sbt("nwT", [P, 4, 128], BF16)
                vn = sbt("vn", [P, 4, 128], BF16)
                S32 = sbt("S32", [P, 4, 128], F32)
                Sb = sbt("Sb", [P, 4, 128], BF16)
                osq = sbt("osq", [P, 4, 128], BF16)
                o32 = sbt("o32", [P, 4, 128], F32)
                msn = sbt("msn", [P, 512], F32)
                rsn = sbt("rsn", [P, 512], F32)
                y1 = sbt("y1", [P, 4, 128], F32)
                ybuf = [sbt("ybuf%d" % i, [P, 4, G], BF16) for i in range(2)]
                chO = fw.dma_chan("go")
                bT = fw.ps(st, "bT", [P, 512], BF16)
                bA, bB_, bC, bD, bE, bF, bG_ = [fw.ps(st, "gb%d" % i, [P, 512], F32) for i in range(7)]
                ones_f = cf.t[:, C_ONE:C_ONE + 128]

                def v4(tk):
                    return tk.t[:].rearrange("p (h i) -> p h i", h=4)

                def mmh(out_tk, l_tk, l_fn, r_tk, r_fn, grp=True):
                    for h in range(4):
                        op("pe", lambda e, h=h: e.matmul(v4(out_tk)[:, h, :], l_fn(h), r_fn(h), start=True, stop=True),
                           reads=[l_tk, r_tk], writes=[out_tk])

                import os
                gstep = int(os.environ.get('K_GSTEP', '99'))
                gi = 0
                for seq in range(n_seq):
                    dma(chg, gsb[:], GS[:, seq * NBS:(seq + 1) * NBS, :], writes=[gsb])
                    op("dve", lambda e: e.memset(S32[:], 0.0), writes=[S32])
                    op("dve", lambda e: e.memset(Sb[:], 0.0), writes=[Sb])
                    for c in range(NBS):
                        tok0 = seq * S + c * P
                        if c % 4 == 0:
                            cur = qkv[gi % 2]
                            yb = ybuf[gi % 2]
                            for t_, src_ in zip(cur, (GQ, GK, GV, GGs)):
                                dma(chq[gi % 2], t_[:], src_[:, :, tok0:tok0 + G].rearrange("h p t -> p h t"), writes=[t_])
                            for t_ in cur:
                                t_.w = (chq[gi % 2], chq[gi % 2].count)
                            gi += 1
                        qT_, kT_, vT_, gg_ = cur
                        co = (c % 4) * P
                        g_ap = gsb[:, c, 0:4]
                        b_ap = gsb[:, c, 4:8]
                        op("dve", lambda e: e.tensor_scalar(ng[:], g_ap, -1.0, None, ALU.mult), reads=[gsb], writes=[ng])
                        op("dve", lambda e: e.tensor_scalar(nbeta[:], b_ap, -1.0, None, ALU.mult), reads=[gsb], writes=[nbeta])
                        op("dve", lambda e: e.tensor_tensor(NG1[:], U4[:], ng[:].unsqueeze(2).to_broadcast([P, 4, 128]), ALU.mult),
                           reads=[U4, ng], writes=[NG1])
                        op("pe", lambda e: e.matmul(bA[:, 0:4], cf[:, C_U:C_U + 128], g_ap, start=True, stop=True), reads=[cf, gsb], writes=[bA])
                        op("pe", lambda e: e.matmul(bA[:, 4:8], cf[:, C_L:C_L + 128], g_ap, start=True, stop=True), reads=[cf, gsb], writes=[bA])
                        op("pe", lambda e: e.matmul(bA[:, 8:12], ones_f, g_ap, start=True, stop=True), reads=[cf, gsb], writes=[bA])
                        op("act", lambda e: e.activation(out=esm[:], in_=bA[:, 0:12], func=AF.Exp), reads=[bA], writes=[esm])
                        if gstep < 2:
                            continue
                        NG1f = NG1.t[:].rearrange("p h i -> p (h i)")
                        op("pe", lambda e: e.matmul(bB_[:], negones[:], NG1f, start=True, stop=True), reads=[negones, NG1], writes=[bB_])
                        op("act", lambda e: e.activation(out=E[:].rearrange("p h i -> p (h i)"), in_=bB_[:], func=AF.Exp), reads=[bB_], writes=[E])
                        op("dve", lambda e: e.tensor_tensor(qdT[:], qT_[:, :, co:co + P], E[:], ALU.mult), reads=[qT_, E], writes=[qdT])
                        if gstep < 3:
                            continue
                        op("pe", lambda e: e.matmul(bC[:], negones[:], NG1f, start=True, stop=False), reads=[negones, NG1], writes=[bC])
                        for h in range(4):
                            op("pe", lambda e, h=h: e.matmul(v4(bC)[:, h, :], NG1[:, h, :], ones_f, start=False, stop=False),
                               reads=[NG1, cf], writes=[bC])
                        op("pe", lambda e: e.matmul(bC[:], ident_bf, nmt4[:].rearrange("p h i -> p (h i)"), start=False, stop=True),
                           reads=[cb, nmt4], writes=[bC])
                        op("act", lambda e: e.activation(out=DT[:].rearrange("p h i -> p (h i)"), in_=bC[:], func=AF.Exp), reads=[bC], writes=[DT])
                        op("pool", lambda e: e.tensor_tensor(DTs[:], DT[:], m014[:], ALU.mult), reads=[DT, m014], writes=[DTs])
                        if gstep < 4:
                            continue
                        mmh(bD, kT_, lambda h: kT_[:, h, co:co + P], kT_, lambda h: kT_[:, h, co:co + P])
                        mmh(bE, kT_, lambda h: kT_[:, h, co:co + P], qT_, lambda h: qT_[:, h, co:co + P])
                        Z, ZA, Pm = Zc[0], ZAc[0], Pc[0]
                        for h in range(4):
                            op("dve", lambda e, h=h, Z=Z: e.scalar_tensor_tensor(
                                out=Z[:, h, :], in0=v4(bD)[:, h, :], scalar=nbeta[:, h:h + 1], in1=DTs[:, h, :],
                                op0=ALU.mult, op1=ALU.mult), reads=[bD, nbeta, DTs], writes=[Z])
                        op("dve", lambda e: e.tensor_tensor(attnT[:].rearrange("p h i -> p (h i)"), bE[:], DT[:].rearrange("p h i -> p (h i)"), ALU.mult),
                           reads=[bE, DT], writes=[attnT])
                        op("pool", lambda e, Z=Z, Pm=Pm: e.tensor_tensor(Pm[:], Z[:], id4[:], ALU.add), reads=[Z, id4], writes=[Pm])
                        if gstep < 5:
                            continue
                        bTv = bT.t[:, 0:512].rearrange("p (h i) -> p h i", h=4)
                        for h in range(4):
                            op("pe", lambda e, h=h, Z=Z: e.transpose(bTv[:, h, :], Z[:, h, :], ident_bf), reads=[Z, cb], writes=[bT])
                        op("act", lambda e, ZA=ZA: e.activation(out=ZA[:], in_=bTv, func=AF.Copy), reads=[bT], writes=[ZA])
                        if gstep < 6:
                            continue
                        for m in range(1, 7):
                            Zn, ZAn, Pn = Zc[m % 2], ZAc[m % 2], Pc[m % 2]
                            mmh(bF, Z, lambda h, Z=Z: Z[:, h, :], ZA, lambda h, ZA=ZA: ZA[:, h, :])
                            op("act", lambda e, ZAn=ZAn: e.activation(out=ZAn[:].rearrange("p h i -> p (h i)"), in_=bF[:], func=AF.Copy),
                               reads=[bF], writes=[ZAn])
                            if m <= 5:
                                mmh(bG_, ZA, lambda h, ZA=ZA: ZA[:, h, :], Z, lambda h, Z=Z: Z[:, h, :])
                                op("dve", lambda e, Zn=Zn: e.tensor_copy(Zn[:].rearrange("p h i -> p (h i)"), bG_[:]), reads=[bG_], writes=[Zn])
                            mmh(bD, ZAn, lambda h, ZAn=ZAn: ZAn[:, h, :], Pm, lambda h, Pm=Pm: Pm[:, h, :])
                            op("dve", lambda e, Pn=Pn, Pm=Pm: e.tensor_tensor(Pn[:].rearrange("p h i -> p (h i)"), bD[:], Pm[:].rearrange("p h i -> p (h i)"), ALU.add),
                               reads=[bD, Pm], writes=[Pn])
                            Z, ZA, Pm = Zn, ZAn, Pn
                        Tt = Pm
                        if gstep < 7:
                            continue
                        bTk = bT.t[:, 0:512].rearrange("p (h i) -> p h i", h=4)
                        for h in range(4):
                            op("pe", lambda e, h=h: e.transpose(bTk[:, h, :], kT_[:, h, co:co + P], ident_bf), reads=[kT_, cb], writes=[bT])
                        op("act", lambda e: e.activation(out=ktk[:], in_=bTk, func=AF.Copy), reads=[bT], writes=[ktk])
                        for h in range(4):
                            op("pe", lambda e, h=h: e.transpose(bTk[:, h, :], vT_[:, h, co:co + P], ident_bf), reads=[vT_, cb], writes=[bT])
                        op("act", lambda e: e.activation(out=vtk[:], in_=bTk, func=AF.Copy), reads=[bT], writes=[vtk])
                        op("dve", lambda e: e.tensor_tensor(kg[:], ktk[:], esm[:, 0:4].unsqueeze(2).to_broadcast([P, 4, 128]), ALU.mult),
                           reads=[ktk, esm], writes=[kg])
                        op("dve", lambda e: e.tensor_tensor(kd[:], ktk[:], esm[:, 4:8].unsqueeze(2).to_broadcast([P, 4, 128]), ALU.mult),
                           reads=[ktk, esm], writes=[kd])
                        if gstep < 8:
                            continue
                        mmh(bB_, kg, lambda h: kg[:, h, :], Tt, lambda h, Tt=Tt: Tt[:, h, :])
                        op("act", lambda e: e.activation(out=nwT[:].rearrange("p h i -> p (h i)"), in_=bB_[:], func=AF.Identity, scale=-1.0),
                           reads=[bB_], writes=[nwT])
                        if gstep < 9:
                            continue
                        for h in range(4):
                            op("pe", lambda e, h=h, Tt=Tt: e.matmul(v4(bC)[:, h, :], Tt[:, h, :], vtk[:, h, :], start=True, stop=False),
                               reads=[Tt, vtk], writes=[bC])
                            op("pe", lambda e, h=h: e.matmul(v4(bC)[:, h, :], nwT[:, h, :], Sb[:, h, :], start=False, stop=True),
                               reads=[nwT, Sb], writes=[bC])
                        op("dve", lambda e: e.tensor_tensor(vn[:], v4(bC), b_ap.unsqueeze(2).to_broadcast([P, 4, 128]), ALU.mult),
                           reads=[bC, gsb], writes=[vn])
                        for h in range(4):
                            op("pe", lambda e, h=h: e.matmul(v4(bE)[:, h, :], Sb[:, h, :], qdT[:, h, :], start=True, stop=False),
                               reads=[Sb, qdT], writes=[bE])
                            op("pe", lambda e, h=h: e.matmul(v4(bE)[:, h, :], vn[:, h, :], attnT[:, h, :], start=False, stop=True),
                               reads=[vn, attnT], writes=[bE])
                        mmh(bF, kd, lambda h: kd[:, h, :], vn, lambda h: vn[:, h, :])
                        op("dve", lambda e: e.tensor_tensor(S32[:], S32[:], esm[:, 8:12].unsqueeze(2).to_broadcast([P, 4, 128]), ALU.mult),
                           reads=[S32, esm], writes=[S32])
                        op("dve", lambda e: e.tensor_tensor(S32[:].rearrange("p h i -> p (h i)"), bF[:], S32[:].rearrange("p h i -> p (h i)"), ALU.add),
                           reads=[bF, S32], writes=[S32])
                        op("act", lambda e: e.activation(out=Sb[:], in_=S32[:], func=AF.Copy), reads=[S32], writes=[Sb])
                        if gstep < 10:
                            continue
                        op("act", lambda e: e.activation(out=osq[:].rearrange("p h i -> p (h i)"), in_=bE[:], func=AF.Square), reads=[bE], writes=[osq])
                        op("act", lambda e: e.activation(out=o32[:].rearrange("p h i -> p (h i)"), in_=bE[:], func=AF.Copy), reads=[bE], writes=[o32])
                        op("pe", lambda e: e.matmul(bG_[:], ones_bf, osq[:].rearrange("p h i -> p (h i)"), start=True, stop=True), reads=[cb, osq], writes=[bG_])
                        op("act", lambda e: e.activation(out=msn[:], in_=bG_[:], func=AF.Ln, bias=epsc[:], scale=1.0 / 128),
                           reads=[bG_, epsc], writes=[msn])
                        op("act", lambda e: e.activation(out=rsn[:], in_=msn[:], func=AF.Exp, scale=-0.5), reads=[msn], writes=[rsn])
                        op("dve", lambda e: e.scalar_tensor_tensor(out=y1[:].rearrange("p h i -> p (h i)"), in0=o32[:].rearrange("p h i -> p (h i)"),
                                                                   scalar=pr["gon"][:, 0:1], in1=rsn[:], op0=ALU.mult, op1=ALU.mult),
                           reads=[o32, pr["gon"], rsn], writes=[y1])
                        op("dve", lambda e, yb=yb: e.tensor_tensor(yb[:, :, co:co + P], y1[:], gg_[:, :, co:co + P], ALU.mult),
                           reads=[y1, gg_], writes=[yb])
                        if c % 4 == 3:
                            tb0 = tok0 - 3 * P
                            fw.dma_st(yb, Y[4:8, :, tb0:tb0 + G].rearrange("h p t -> p h t"), yb[:])
            fw.barrier()

        import os
        stop = int(os.environ.get("K_STOP", "99"))
        for l in range(depth):
            if stop <= 0:
                break
            if l == 0:
                group_pass(xT, None, None, (1, 0), 0, False)
            else:
                group_pass(XR, l - 1, (2, l - 1), (1, l), l, False)
            if stop <= 1:
                break
            fox_phase()
            if stop <= 2:
                break
            gdn_phase(l)
        if stop > 3:
            group_pass(XR, depth - 1, (2, depth - 1), None, None, True)
    return nc


def host_inputs(inputs, n_cores, n_seq, S):
    consts = make_consts()
    in_maps = []
    x = np.asarray(inputs["x"], np.float32)
    depth = int(np.asarray(inputs["w_in"]).shape[0])
    big = ("ffn1_w_in", "ffn1_w_out", "ffn2_w_in", "ffn2_w_out", "w_in", "w_out")
    shared = {k: np.ascontiguousarray(np.asarray(inputs[k], np.float32)) for k in big}
    shared["pvec"] = make_pvec(inputs, depth)
    for c in range(n_cores):
        xs = x[c * n_seq:(c + 1) * n_seq].reshape(n_seq * S, D)
        xT = np.ascontiguousarray(xs.T).reshape(KC, P, n_seq * S)
        m = {"xT": xT, "consts": consts}
        m.update(shared)
        in_maps.append(m)
    return in_maps


def kernel(**inputs):
    x = np.asarray(inputs["x"])
    B, S, _ = x.shape
    depth = int(np.asarray(inputs["w_in"]).shape[0])
    n_seq = B // N_CORES
    nc = bass.Bass("TRN2", target_bir_lowering=False)
    build(nc, n_seq, S, depth)
    in_maps = host_inputs(inputs, N_CORES, n_seq, S)
    res = run_bass_kernel_spmd(nc, in_maps, core_ids=list(range(N_CORES)))
    out = np.empty((B, S, D), np.float32)
    for c in range(N_CORES):
        oT = np.asarray(res.results[c]["outT"]).reshape(D, n_seq * S)
        out[c * n_seq:(c + 1) * n_seq] = oT.T.reshape(n_seq, S, D)
    return out
```

```python
import contextlib
import os
import numpy as np
import concourse.bass as bass
import concourse.mybir as mybir
from concourse.bass_utils import run_bass_kernel_spmd

F32 = mybir.dt.float32
BF16 = mybir.dt.bfloat16
AF = mybir.ActivationFunctionType
ALU = mybir.AluOpType

P = 128
D = 1024
KC = 8
DFF = 2816
NJ = 22
NIN = 3600
G = 512
EPS = 1e-6
NEG = -1.0e30
N_CORES = 8

O_FQ, O_FK, O_FV, O_FF = 0, 512, 1024, 1536
O_GQ, O_GK, O_GV, O_GA, O_GB, O_GG = 1544, 2056, 2568, 3080, 3084, 3088

C_ID, C_U, C_L, C_ONE, C_BD, C_NMT, C_M01, C_FM = 0, 128, 256, 384, 512, 640, 768, 896
NCONST = 896 + 2048
PV_N1, PV_N2, PV_NM, PV_GQ, PV_GK, PV_CW, PV_GON, PV_FB, PV_DT, PV_AL = 0, 8, 16, 24, 25, 26, 74, 75, 76, 80
NPV = 84


def make_pvec(inputs, depth):
    pv = np.zeros((depth, 128, NPV), np.float32)
    p = np.arange(128)
    for l in range(depth):
        pv[l, :, PV_N1:PV_N1 + 8] = np.asarray(inputs["ffn1_norm"][l], np.float32).reshape(8, 128).T
        pv[l, :, PV_N2:PV_N2 + 8] = np.asarray(inputs["ffn2_norm"][l], np.float32).reshape(8, 128).T
        pv[l, :, PV_NM:PV_NM + 8] = np.asarray(inputs["mix_norm"][l], np.float32).reshape(8, 128).T
        pv[l, :, PV_GQ] = np.asarray(inputs["fox_q_norm"][l], np.float32)[p % 64]
        pv[l, :, PV_GK] = np.asarray(inputs["fox_k_norm"][l], np.float32)[p % 64]
        cw = np.asarray(inputs["gdn_conv"][l], np.float32)
        pv[l, :, PV_CW:PV_CW + 48] = cw.reshape(4, 12, 128).transpose(2, 1, 0).reshape(128, 48)
        pv[l, :, PV_GON] = np.asarray(inputs["gdn_out_norm"][l], np.float32)
        pv[l, 0:8, PV_FB] = np.asarray(inputs["fox_f_bias"][l], np.float32)
        pv[l, :, PV_DT:PV_DT + 4] = np.asarray(inputs["gdn_dt_bias"][l], np.float32)[None, :]
        pv[l, :, PV_AL:PV_AL + 4] = np.asarray(inputs["gdn_a_log"][l], np.float32)[None, :]
    return pv


def make_consts():
    c = np.zeros((128, NCONST), np.float32)
    r = np.arange(128)
    pp, ff = r[:, None], r[None, :]
    c[:, C_ID:C_ID + 128] = (pp == ff)
    c[:, C_U:C_U + 128] = (pp <= ff)
    c[:, C_L:C_L + 128] = (pp > ff)
    c[:, C_ONE:C_ONE + 128] = 1.0
    c[:, C_BD:C_BD + 128] = ((pp // 64) == (ff // 64))
    c[:, C_NMT:C_NMT + 128] = np.where(ff < pp, NEG, 0.0)
    c[:, C_M01:C_M01 + 128] = (ff > pp)
    t = np.arange(512)[None, :]
    for rr in range(4):
        c[:, C_FM + rr * 512:C_FM + (rr + 1) * 512] = np.where((rr * 128 + pp) > t, NEG, 0.0)
    return c


class Tk:
    __slots__ = ("t", "name", "w", "rs")

    def __init__(self, t, name):
        self.t = t
        self.name = name
        self.w = None
        self.rs = []

    def __getitem__(self, k):
        return self.t[k]


class Chan:
    def __init__(self, sem, name):
        self.sem = sem
        self.count = 0
        self.name = name


class FW:
    def __init__(self, nc, stack):
        self.nc = nc
        self.stack = stack
        self.engs = {}
        for nm, e in (("pe", nc.tensor), ("act", nc.scalar), ("dve", nc.vector),
                      ("pool", nc.gpsimd), ("sp", nc.sync)):
            sem = stack.enter_context(nc.semaphore("s_" + nm))
            self.engs[nm] = (e, Chan(sem, nm))
        self.waited = {nm: {} for nm in self.engs}
        self.chans = []
        self.uid = 0

    def sb(self, st, name, shape, dt):
        self.uid += 1
        t = st.enter_context(self.nc.sbuf_tensor("%s_%d" % (name, self.uid), list(shape), dt))
        return Tk(t, name)

    def ps(self, st, name, shape, dt=F32):
        self.uid += 1
        t = st.enter_context(self.nc.psum_tensor("%s_%d" % (name, self.uid), list(shape), dt))
        return Tk(t, name)

    def dma_chan(self, name):
        for c in self.chans:
            if c.name == name:
                return c
        sem = self.stack.enter_context(self.nc.semaphore("d_" + name))
        c = Chan(sem, name)
        self.chans.append(c)
        return c

    def _need(self, en, reads, writes):
        need = {}

        def add(dep):
            if dep is None:
                return
            ch, tk = dep
            if need.get(ch, 0) < tk:
                need[ch] = tk

        my = self.engs[en][1]
        for r in reads:
            add(r.w)
        for w in writes:
            add(w.w)
            for d in w.rs:
                if d[0] is my:
                    continue
                add(d)
        return need

    def _waits(self, en, need, skip_self_pe):
        eng, ch = self.engs[en]
        wd = self.waited[en]
        for c, tk in need.items():
            if skip_self_pe and c is ch:
                continue
            if wd.get(c, 0) >= tk:
                continue
            eng.wait_ge(c.sem, tk)
            wd[c] = tk

    def _record(self, dep, reads, writes):
        for w in writes:
            w.w = dep
            w.rs = []
        for r in reads:
            r.rs.append(dep)
            if len(r.rs) > 16:
                m = {}
                for c, t in r.rs:
                    if m.get(c, 0) < t:
                        m[c] = t
                r.rs = list(m.items())

    def op(self, en, fn, reads=(), writes=()):
        eng, ch = self.engs[en]
        self._waits(en, self._need(en, reads, writes), en == "pe")
        ins = fn(eng)
        ch.count += 1
        ins.then_inc(ch.sem, 1)
        self._record((ch, ch.count), reads, writes)
        return ins

    def dma(self, chan, out_ap, in_ap, reads=(), writes=(), en="sp"):
        eng, _ = self.engs[en]
        self._waits(en, self._need(en, reads, writes), False)
        ins = eng.dma_start(out=out_ap, in_=in_ap)
        chan.count += 16
        ins.then_inc(chan.sem, 16)
        self._record((chan, chan.count), reads, writes)
        return ins

    def dma_st(self, src_tk, out_ap, in_ap, en="sp"):
        return self.dma(self.dma_chan("o_" + src_tk.name), out_ap, in_ap, reads=[src_tk], en=en)

    def barrier(self):
        chs = [c for (_, c) in self.engs.values()] + self.chans
        for en, (eng, mych) in self.engs.items():
            wd = self.waited[en]
            for c in chs:
                if c is mych or c.count == 0:
                    continue
                if wd.get(c, 0) >= c.count:
                    continue
                eng.wait_ge(c.sem, c.count)
                wd[c] = c.count


class Stream:
    def __init__(self, fw, st, nbuf=6, look=4):
        self.fw = fw
        self.bufs = [fw.sb(st, "wbuf%d" % i, [P, 4096], BF16) for i in range(nbuf)]
        self.chs = [fw.dma_chan("wb%d" % i) for i in range(nbuf)]
        self.look = look
        self.plan = []
        self.issued = 0
        self.taken = 0

    def add(self, dram_ap, shape, parts=None):
        self.plan.append((dram_ap, shape, parts))

    def _issue(self, i):
        ap, shape, parts = self.plan[i]
        b = i % len(self.bufs)
        dst = self.view(self.bufs[b], shape)
        if parts is None:
            self.fw.dma(self.chs[b], dst, ap, writes=[self.bufs[b]])
        else:
            for src_ap, sel in parts:
                self.fw.dma(self.chs[b], sel(dst), src_ap, writes=[self.bufs[b]])

    @staticmethod
    def view(buf, shape):
        n = int(np.prod(shape[1:]))
        v = buf.t[:, 0:n]
        if len(shape) == 3:
            v = v.rearrange("p (a b) -> p a b", a=shape[1])
        elif len(shape) == 4:
            v = v.rearrange("p (a b c) -> p a b c", a=shape[1], b=shape[2])
        return v

    def next(self):
        i = self.taken
        while self.issued < min(len(self.plan), i + 1 + self.look):
            self._issue(self.issued)
            self.issued += 1
        self.taken += 1
        b = i % len(self.bufs)
        return self.bufs[b], self.view(self.bufs[b], self.plan[i][1])


def build(nc, n_seq, S, depth, dbg=False):
    T = n_seq * S
    NG = T // G
    NB = T // P
    NBS = S // P
    GPS = S // G

    def din(name, shape):
        return nc.dram_tensor(name, list(shape), F32, kind="ExternalInput").ap()

    def dscr(name, shape, dt):
        return nc.dram_tensor(name, list(shape), dt).ap()

    xT = din("xT", [KC, P, T])
    outT = nc.dram_tensor("outT", [KC, P, T], F32, kind="ExternalOutput").ap()
    consts_d = din("consts", [P, NCONST])
    pvec_d = din("pvec", [depth, P, NPV])
    w_ffn_in = {1: din("ffn1_w_in", [depth, D, 2 * DFF]), 2: din("ffn2_w_in", [depth, D, 2 * DFF])}
    w_ffn_out = {1: din("ffn1_w_out", [depth, DFF, D]), 2: din("ffn2_w_out", [depth, DFF, D])}
    w_in = din("w_in", [depth, D, NIN])
    w_out = din("w_out", [depth, D, D])

    wb_ffn_in = {f: dscr("wb_ffn%d_in" % f, [depth, D, 2 * DFF], BF16) for f in (1, 2)}
    wb_ffn_out = {f: dscr("wb_ffn%d_out" % f, [depth, DFF, D], BF16) for f in (1, 2)}
    wb_in = dscr("wb_in", [depth, D, NIN], BF16)
    wb_out = dscr("wb_out", [depth, D, D], BF16)
    XR = dscr("XR", [KC, P, T], F32)
    QF = dscr("QF", [n_seq, 8, 70, S], BF16)
    KF = dscr("KF", [n_seq, 8, 70, S], BF16)
    VF = dscr("VF", [P, NB, 520], BF16)
    GQ = dscr("GQ", [4, P, T], BF16)
    GK = dscr("GK", [4, P, T], BF16)
    GV = dscr("GV", [4, P, T], BF16)
    GGs = dscr("GGs", [4, P, T], BF16)
    GS = dscr("GS", [P, NB, 8], F32)
    Y = dscr("Y", [KC, P, T], BF16)

    with contextlib.ExitStack() as top:
        fw = FW(nc, top)
        top.enter_context(nc.Block())
        op, dma = fw.op, fw.dma

        cf = fw.sb(top, "cf", [P, C_FM], F32)
        cb = fw.sb(top, "cb", [P, NCONST], BF16)
        ch_c = fw.dma_chan("c")
        dma(ch_c, cf[:], consts_d[:, 0:C_FM], writes=[cf])
        op("dve", lambda e: e.tensor_copy(cb[:, 0:C_FM], cf[:]), reads=[cf], writes=[cb])
        with contextlib.ExitStack() as tst:
            ctmp = fw.sb(tst, "ctmp", [P, 2048], F32)
            dma(fw.dma_chan("c2"), ctmp[:], consts_d[:, C_FM:NCONST], writes=[ctmp])
            op("dve", lambda e: e.tensor_copy(cb[:, C_FM:NCONST], ctmp[:]), reads=[ctmp], writes=[cb])
            fw.barrier()
        negones = fw.sb(top, "negones", [P, 128], F32)
        op("pool", lambda e: e.memset(negones[:], -1.0), writes=[negones])
        epsc = fw.sb(top, "epsc", [P, 1], F32)
        op("pool", lambda e: e.memset(epsc[:], EPS), writes=[epsc])
        onec = fw.sb(top, "onec", [P, 1], F32)
        op("pool", lambda e: e.memset(onec[:], 1.0), writes=[onec])
        ones_bf = cb.t[:, C_ONE:C_ONE + 128]
        ident_bf = cb.t[:, C_ID:C_ID + 128]

        ch_p = fw.dma_chan("p")
        prm = {}
        pv_tiles = []
        for l in range(depth):
            pv_t = fw.sb(top, "pvec%d" % l, [P, NPV], F32)
            dma(ch_p, pv_t[:], pvec_d[l], writes=[pv_t])
            pv_tiles.append(pv_t)
        fw.barrier()
        for l in range(depth):
            pv_t = pv_tiles[l]
            d = {}

            class _V:
                def __init__(self, tk, c0, shape=None):
                    self.tk, self.c0, self.shape = tk, c0, shape
            gq = fw.sb(top, "gq%d" % l, [P, 1], F32)
            op("dve", lambda e: e.tensor_scalar(gq[:], pv_t[:, PV_GQ:PV_GQ + 1], 0.125, None, ALU.mult), reads=[pv_t], writes=[gq])
            gk = fw.sb(top, "gk%d" % l, [P, 1], F32)
            op("dve", lambda e: e.tensor_copy(gk[:], pv_t[:, PV_GK:PV_GK + 1]), reads=[pv_t], writes=[gk])
            nfb = fw.sb(top, "nfb%d" % l, [8, 1], F32)
            op("dve", lambda e: e.tensor_scalar(nfb[:], pv_t[0:8, PV_FB:PV_FB + 1], -1.0, None, ALU.mult), reads=[pv_t], writes=[nfb])
            dtb = fw.sb(top, "dtb%d" % l, [P, 4, 4], F32)
            nA = fw.sb(top, "nA%d" % l, [P, 4, 4], F32)
            eA = fw.sb(top, "eA%d" % l, [P, 4], F32)
            op("act", lambda e: e.activation(out=eA[:], in_=pv_t[:, PV_AL:PV_AL + 4], func=AF.Exp), reads=[pv_t], writes=[eA])
            for tb in range(4):
                op("dve", lambda e, tb=tb: e.tensor_copy(dtb[:, tb, :], pv_t[:, PV_DT:PV_DT + 4]), reads=[pv_t], writes=[dtb])
                op("dve", lambda e, tb=tb: e.tensor_scalar(nA[:, tb, :], eA[:], -1.0, None, ALU.mult), reads=[eA], writes=[nA])
            n1 = fw.sb(top, "n1_%d" % l, [P, KC], F32)
            n2 = fw.sb(top, "n2_%d" % l, [P, KC], F32)
            nm = fw.sb(top, "nm_%d" % l, [P, KC], F32)
            cw = fw.sb(top, "cw%d" % l, [P, 12, 4], F32)
            gon = fw.sb(top, "gon%d" % l, [P, 1], F32)
            op("dve", lambda e: e.tensor_copy(n1[:], pv_t[:, PV_N1:PV_N1 + 8]), reads=[pv_t], writes=[n1])
            op("dve", lambda e: e.tensor_copy(n2[:], pv_t[:, PV_N2:PV_N2 + 8]), reads=[pv_t], writes=[n2])
            op("dve", lambda e: e.tensor_copy(nm[:], pv_t[:, PV_NM:PV_NM + 8]), reads=[pv_t], writes=[nm])
            op("dve", lambda e: e.tensor_copy(cw[:].rearrange("p c k -> p (c k)"), pv_t[:, PV_CW:PV_CW + 48]), reads=[pv_t], writes=[cw])
            op("dve", lambda e: e.tensor_copy(gon[:], pv_t[:, PV_GON:PV_GON + 1]), reads=[pv_t], writes=[gon])
            d.update(n1=n1, n2=n2, nm=nm, gq=gq, gk=gk, nfb=nfb, cw=cw, dtb=dtb, nA=nA, gon=gon)
            prm[l] = d

        ch_w = fw.dma_chan("w")
        for l in range(depth):
            for f in (1, 2):
                for r0 in range(0, D, 256):
                    dma(ch_w, wb_ffn_in[f][l, r0:r0 + 256, :], w_ffn_in[f][l, r0:r0 + 256, :], en="pool")
                for r0 in range(0, DFF, 704):
                    dma(ch_w, wb_ffn_out[f][l, r0:r0 + 704, :], w_ffn_out[f][l, r0:r0 + 704, :], en="pool")
            for r0 in range(0, D, 512):
                dma(ch_w, wb_in[l, r0:r0 + 512, :], w_in[l, r0:r0 + 512, :], en="pool")
                dma(ch_w, wb_out[l, r0:r0 + 512, :], w_out[l, r0:r0 + 512, :], en="pool")
        with contextlib.ExitStack() as tst:
            onesrow = fw.sb(tst, "onesrow", [8, 3, G], BF16)
            op("pool", lambda e: e.memset(onesrow[:], 1.0), writes=[onesrow])
            for sq_ in range(n_seq):
                for tq in range(0, S, G):
                    dma(ch_w, QF[sq_, :, 67:70, tq:tq + G], onesrow[:], reads=[onesrow])
                    dma(ch_w, KF[sq_, :, 64:67, tq:tq + G], onesrow[:], reads=[onesrow])
            fw.barrier()

        def group_pass(src, l_mix, ffn_a, ffn_b, l_proj, final):
            with contextlib.ExitStack() as st:
                X = [fw.sb(st, "X%d" % i, [P, KC, G], F32) for i in range(2)]
                chX = [fw.dma_chan("x%d" % i) for i in range(2)]
                chY = fw.dma_chan("y")
                xn = fw.sb(st, "xn", [P, KC, G], BF16)
                Yb = xn
                H = fw.sb(st, "H", [P, NJ, G], BF16)
                sq = H
                msb = [fw.sb(st, "ms%d" % i, [P, G], F32) for i in range(2)]
                rsb = [fw.sb(st, "rstd%d" % i, [P, G], F32) for i in range(2)]
                rs_i = [0]
                sgt = [fw.sb(st, "sgt%d" % i, [P, G], F32) for i in range(2)]
                bank = [fw.ps(st, "bk%d" % i, [P, G], F32) for i in range(8)]
                bg, bu, bo, bm = bank[0:2], bank[2:4], bank[4:6], bank[6:8]
                chO = fw.dma_chan("o")
                stream = Stream(fw, st, nbuf=5, look=3)
                if l_proj is not None:
                    wsm = fw.sb(st, "wsm", [P, KC, 16], BF16)
                    chS = fw.dma_chan("s")
                    wv = wb_in[l_proj].rearrange("(kc p) c -> p kc c", p=P)
                    dma(chS, wsm[:, :, 0:8], wv[:, :, O_FF:O_FF + 8], writes=[wsm])
                    dma(chS, wsm[:, :, 8:16], wv[:, :, O_GA:O_GA + 8], writes=[wsm])
                    cbuf = [fw.sb(st, "cbuf%d" % i, [P, G + 3], F32) for i in range(12)]
                    acc = [fw.sb(st, "acc%d" % i, [P, G], F32) for i in range(2)]
                    sqb = [fw.sb(st, "sqb%d" % i, [P, G], BF16) for i in range(2)]
                    ob = [fw.sb(st, "ob%d" % i, [P, G], BF16) for i in range(4)]
                    vt = [fw.sb(st, "vt%d" % i, [P, 4, 8, 65], BF16) for i in range(2)]
                    for v in vt:
                        op("pool", lambda e, v=v: e.memset(v[:], 1.0), writes=[v])
                    sm_e = fw.sb(st, "sm_e", [8, G], F32)
                    sm_l = sm_e
                    cum = fw.sb(st, "cum", [8, G], F32)
                    carry = fw.sb(st, "carry", [8, 1], F32)
                    r32 = fw.sb(st, "r32", [8, G], F32)
                    t32 = fw.sb(st, "t32", [8, G], F32)
                    ex = [fw.sb(st, "ex0", [8, 6, G], BF16)] * 2
                    onesr = fw.sb(st, "onesr", [8, G], F32)
                    op("pool", lambda e: e.memset(onesr[:], 1.0), writes=[onesr])
                    tz = fw.sb(st, "tz", [P, 4, 8], F32)
                    te = fw.sb(st, "te", [P, 4, 8], F32)
                    gs = [fw.sb(st, "gs%d" % i, [P, 4, 8], F32) for i in range(2)]

                def plan_ffn(f, l):
                    wi = wb_ffn_in[f][l].rearrange("(kc p) (gu c) -> p kc gu c", p=P, gu=2)
                    for jb in range(NJ // 2):
                        stream.add(None, [P, KC, 2, 256], parts=[
                            (wi[:, :, 0, jb * 256:(jb + 1) * 256], lambda v: v[:, :, 0, :]),
                            (wi[:, :, 1, jb * 256:(jb + 1) * 256], lambda v: v[:, :, 1, :])])
                    wo = wb_ffn_out[f][l].rearrange("(j p) c -> p j c", p=P)
                    for m in range(KC):
                        stream.add(wo[:, :, m * 128:(m + 1) * 128], [P, NJ, 128])

                for g in range(NG):
                    if l_mix is not None:
                        wm = wb_out[l_mix].rearrange("(kc p) c -> p kc c", p=P)
                        for hf in range(2):
                            stream.add(wm[:, :, hf * 512:(hf + 1) * 512], [P, KC, 512])
                    if ffn_a is not None:
                        plan_ffn(*ffn_a)
                    if ffn_b is not None:
                        plan_ffn(*ffn_b)
                    if l_proj is not None:
                        wv = wb_in[l_proj].rearrange("(kc p) c -> p kc c", p=P)
                        for o in (O_GQ, O_GK, O_GV, O_GG, O_FQ, O_FK, O_FV):
                            stream.add(wv[:, :, o:o + 512], [P, KC, 512])

                def mm(out_tk, out_ap, l_tk, l_ap, r_tk, r_ap, start, stop):
                    op("pe", lambda e: e.matmul(out_ap, l_ap, r_ap, start=start, stop=stop),
                       reads=[l_tk, r_tk], writes=[out_tk])

                def rsqrt_from(ps_tk, ps_ap, scale):
                    ms, rstd = msb[rs_i[0] % 2], rsb[rs_i[0] % 2]
                    rs_i[0] += 1
                    op("act", lambda e: e.activation(out=ms[:], in_=ps_ap, func=AF.Ln, bias=epsc[:], scale=scale),
                       reads=[ps_tk, epsc], writes=[ms])
                    op("act", lambda e: e.activation(out=rstd[:], in_=ms[:], func=AF.Exp, scale=-0.5),
                       reads=[ms], writes=[rstd])
                    return rstd

                def rmsnorm(Xt, wvec):
                    op("act", lambda e: e.activation(out=sq[:, 0:KC, :], in_=Xt[:], func=AF.Square), reads=[Xt], writes=[sq])
                    b = bm[0]
                    for kc in range(KC):
                        mm(b, b[:], cb, ones_bf, sq, sq[:, kc, :], kc == 0, kc == KC - 1)
                    rstd = rsqrt_from(b, b[:], 1.0 / D)
                    for kc in range(KC):
                        op("dve", lambda e, kc=kc: e.scalar_tensor_tensor(
                            out=xn[:, kc, :], in0=Xt[:, kc, :], scalar=wvec[:, kc:kc + 1], in1=rstd[:],
                            op0=ALU.mult, op1=ALU.mult), reads=[Xt, wvec, rstd], writes=[xn])

                def ffn(Xt, f, l):
                    rmsnorm(Xt, prm[l]["n%d" % f])
                    for jb in range(NJ // 2):
                        wtk, wp = stream.next()
                        for jj in range(2):
                            j = jb * 2 + jj
                            pg, pu = bg[j % 2], bu[j % 2]
                            for kc in range(KC):
                                mm(pg, pg[:], wtk, wp[:, kc, 0, jj * 128:(jj + 1) * 128], xn, xn[:, kc, :], kc == 0, kc == KC - 1)
                            for kc in range(KC):
                                mm(pu, pu[:], wtk, wp[:, kc, 1, jj * 128:(jj + 1) * 128], xn, xn[:, kc, :], kc == 0, kc == KC - 1)
                            sg = sgt[j % 2]
                            op("act", lambda e, pg=pg, sg=sg: e.activation(out=sg[:], in_=pg[:], func=AF.Silu),
                               reads=[pg], writes=[sg])
                            op("dve", lambda e, pu=pu, sg=sg, j=j: e.tensor_tensor(H[:, j, :], pu[:], sg[:], ALU.mult),
                               reads=[pu, sg], writes=[H])
                    for m in range(KC):
                        wtk, wp = stream.next()
                        po = bo[m % 2]
                        for j in range(NJ):
                            mm(po, po[:], wtk, wp[:, j, :], H, H[:, j, :], j == 0, j == NJ - 1)
                        op("dve", lambda e, po=po, m=m: e.scalar_tensor_tensor(
                            out=Xt[:, m, :], in0=po[:], scalar=0.5, in1=Xt[:, m, :], op0=ALU.mult, op1=ALU.add),
                            reads=[po, Xt], writes=[Xt])

                dma(chX[0], X[0][:], src[:, :, 0:G].rearrange("kc p t -> p kc t"), writes=[X[0]])
                for g in range(NG):
                    Xt = X[g % 2]
                    t0 = g * G
                    seq, ts0 = t0 // S, t0 % S
                    if g + 1 < NG:
                        dma(chX[(g + 1) % 2], X[(g + 1) % 2][:],
                            src[:, :, t0 + G:t0 + 2 * G].rearrange("kc p t -> p kc t"), writes=[X[(g + 1) % 2]])
                    if l_mix is not None:
                        dma(chY, Yb[:], Y[:, :, t0:t0 + G].rearrange("kc p t -> p kc t"), writes=[Yb])
                        for hf in range(2):
                            wtk, wp = stream.next()
                            for mm_ in range(4):
                                m = hf * 4 + mm_
                                po = bo[m % 2]
                                for kc in range(KC):
                                    mm(po, po[:], wtk, wp[:, kc, mm_ * 128:(mm_ + 1) * 128], Yb, Yb[:, kc, :], kc == 0, kc == KC - 1)
                                op("dve", lambda e, po=po, m=m: e.tensor_tensor(Xt[:, m, :], po[:], Xt[:, m, :], ALU.add),
                                   reads=[po, Xt], writes=[Xt])
                    if ffn_a is not None:
                        ffn(Xt, *ffn_a)
                    if ffn_b is not None:
                        ffn(Xt, *ffn_b)
                    if final:
                        fw.dma_st(Xt, outT[:, :, t0:t0 + G].rearrange("kc p t -> p kc t"), Xt[:], en="pool")
                        continue
                    l = l_proj
                    pr = prm[l]
                    fw.dma_st(Xt, XR[:, :, t0:t0 + G].rearrange("kc p t -> p kc t"), Xt[:], en="pool")
                    rmsnorm(Xt, pr["nm"])
                    blk0 = t0 // P
                    if ts0 == 0:
                        for ci in range(12):
                            op("pool", lambda e, ci=ci: e.memset(cbuf[ci][:, 0:3], 0.0), writes=[cbuf[ci]])
                    for gi, (dst, kind) in enumerate(((GQ, "q"), (GK, "k"), (GV, "v"), (GGs, "g"))):
                        wtk, wp = stream.next()
                        for c in range(4):
                            pb = bg[c % 2]
                            for kc in range(KC):
                                mm(pb, pb[:], wtk, wp[:, kc, c * 128:(c + 1) * 128], xn, xn[:, kc, :], kc == 0, kc == KC - 1)
                            o_t = ob[c % 4]
                            if kind == "g":
                                op("act", lambda e, pb=pb, o_t=o_t: e.activation(out=o_t[:], in_=pb[:], func=AF.Silu),
                                   reads=[pb], writes=[o_t])
                                fw.dma_st(o_t, dst[c, :, t0:t0 + G], o_t[:], en="pool")
                                continue
                            ci = gi * 4 + c
                            cbt = cbuf[ci]
                            a = acc[c % 2]
                            op("act", lambda e, pb=pb, cbt=cbt: e.activation(out=cbt[:, 3:3 + G], in_=pb[:], func=AF.Copy),
                               reads=[pb], writes=[cbt])
                            op("dve", lambda e, a=a, ci=ci, cbt=cbt: e.tensor_scalar(a[:], cbt[:, 3:3 + G], pr["cw"][:, ci, 3:4], None, ALU.mult),
                               reads=[cbt, pr["cw"]], writes=[a])
                            for k in (2, 1, 0):
                                op("dve", lambda e, a=a, ci=ci, k=k, cbt=cbt: e.scalar_tensor_tensor(
                                    out=a[:], in0=cbt[:, k:k + G], scalar=pr["cw"][:, ci, k:k + 1], in1=a[:],
                                    op0=ALU.mult, op1=ALU.add), reads=[cbt, pr["cw"], a], writes=[a])
                            op("pool", lambda e, cbt=cbt: e.tensor_copy(cbt[:, 0:3], cbt[:, G:G + 3]),
                               reads=[cbt], writes=[cbt])
                            if kind == "v":
                                op("act", lambda e, a=a, o_t=o_t: e.activation(out=o_t[:], in_=a[:], func=AF.Silu),
                                   reads=[a], writes=[o_t])
                                fw.dma_st(o_t, dst[c, :, t0:t0 + G], o_t[:], en="pool")
                            else:
                                op("act", lambda e, a=a, cbt=cbt: e.activation(out=cbt[:, 3:3 + G], in_=a[:], func=AF.Silu),
                                   reads=[a], writes=[cbt])
                    for gi, (dst, kind) in enumerate(((GQ, "q"), (GK, "k"))):
                        for c in range(4):
                            ci = gi * 4 + c
                            cbt = cbuf[ci]
                            sb_ = sqb[c % 2]
                            o_t = ob[c % 4]
                            op("act", lambda e, cbt=cbt, sb_=sb_: e.activation(out=sb_[:], in_=cbt[:, 3:3 + G], func=AF.Square),
                               reads=[cbt], writes=[sb_])
                            pn = bm[c % 2]
                            mm(pn, pn[:], cb, ones_bf, sb_, sb_[:], True, True)
                            rstd = rsqrt_from(pn, pn[:], 1.0)
                            qs = (128.0 ** -0.5) if kind == "q" else 1.0
                            op("dve", lambda e, cbt=cbt, o_t=o_t, qs=qs, rstd=rstd: e.scalar_tensor_tensor(
                                out=o_t[:], in0=cbt[:, 3:3 + G], scalar=qs, in1=rstd[:], op0=ALU.mult, op1=ALU.mult),
                                reads=[cbt, rstd], writes=[o_t])
                            fw.dma_st(o_t, dst[c, :, t0:t0 + G], o_t[:], en="pool")
                    for dst, gvec in ((QF, pr["gq"]), (KF, pr["gk"])):
                        wtk, wp = stream.next()
                        for c in range(4):
                            pb = bg[c % 2]
                            for kc in range(KC):
                                mm(pb, pb[:], wtk, wp[:, kc, c * 128:(c + 1) * 128], xn, xn[:, kc, :], kc == 0, kc == KC - 1)
                            sb_ = sqb[c % 2]
                            op("act", lambda e, pb=pb, sb_=sb_: e.activation(out=sb_[:], in_=pb[:], func=AF.Square), reads=[pb], writes=[sb_])
                            pn = bm[c % 2]
                            mm(pn, pn[:], cb, cb.t[:, C_BD:C_BD + 128], sb_, sb_[:], True, True)
                            rstd = rsqrt_from(pn, pn[:], 1.0 / 64)
                            o_t = ob[c % 4]
                            op("dve", lambda e, pb=pb, o_t=o_t, gvec=gvec, rstd=rstd: e.scalar_tensor_tensor(
                                out=o_t[:], in0=pb[:], scalar=gvec[:, 0:1], in1=rstd[:], op0=ALU.mult, op1=ALU.mult),
                                reads=[pb, gvec, rstd], writes=[o_t])
                            for hh in range(2):
                                fw.dma_st(o_t, dst[seq, 2 * c + hh, 0:64, ts0:ts0 + G], o_t[hh * 64:(hh + 1) * 64, :], en="pool")
                    wtk, wp = stream.next()
                    vtile = vt[g % 2]
                    for tb in range(4):
                        pb = bu[tb % 2]
                        for kc in range(KC):
                            mm(pb, pb[:], xn, xn[:, kc, tb * 128:(tb + 1) * 128], wtk, wp[:, kc, :], kc == 0, kc == KC - 1)
                        op("act", lambda e, pb=pb, tb=tb: e.activation(
                            out=vtile[:, tb, :, 0:64], in_=pb[:].rearrange("p (h d) -> p h d", h=8), func=AF.Copy),
                            reads=[pb], writes=[vtile])
                    fw.dma_st(vtile, VF[:, blk0:blk0 + 4, :], vtile[:].rearrange("p a h d -> p a (h d)"), en="pool")
                    pb = bm[1]
                    for kc in range(KC):
                        mm(pb, pb[0:16, :], wsm, wsm[:, kc, :], xn, xn[:, kc, :], kc == 0, kc == KC - 1)
                    op("act", lambda e: e.activation(out=sm_e[:], in_=pb[0:8, :], func=AF.Exp, bias=pr["nfb"][:], scale=-1.0),
                       reads=[pb, pr["nfb"]], writes=[sm_e])
                    op("act", lambda e: e.activation(out=sm_l[:], in_=sm_e[:], func=AF.Ln, bias=onec[0:8, :], scale=1.0),
                       reads=[sm_e, onec], writes=[sm_l])
                    if ts0 == 0:
                        op("dve", lambda e: e.memset(carry[:], 0.0), writes=[carry])
                    op("dve", lambda e: e.tensor_tensor_scan(cum[:], onesr[:], sm_l[:], carry[:], ALU.mult, ALU.subtract),
                       reads=[onesr, sm_l, carry], writes=[cum])
                    op("dve", lambda e: e.tensor_copy(carry[:], cum[:, G - 1:G]), reads=[cum], writes=[carry])
                    ext = ex[g % 2]
                    op("dve", lambda e: e.tensor_copy(ext[:, 0, :], cum[:]), reads=[cum], writes=[ext])
                    op("dve", lambda e: e.tensor_copy(t32[:], ext[:, 0, :]), reads=[ext], writes=[t32])
                    op("dve", lambda e: e.tensor_tensor(r32[:], cum[:], t32[:], ALU.subtract), reads=[cum, t32], writes=[r32])
                    op("dve", lambda e: e.tensor_copy(ext[:, 1, :], r32[:]), reads=[r32], writes=[ext])
                    op("dve", lambda e: e.tensor_copy(t32[:], ext[:, 1, :]), reads=[ext], writes=[t32])
                    op("dve", lambda e: e.tensor_tensor(r32[:], r32[:], t32[:], ALU.subtract), reads=[r32, t32], writes=[r32])
                    op("dve", lambda e: e.tensor_copy(ext[:, 2, :], r32[:]), reads=[r32], writes=[ext])
                    op("dve", lambda e: e.tensor_scalar(ext[:, 3:6, :], ext[:, 0:3, :], -1.0, None, ALU.mult), reads=[ext], writes=[ext])
                    fw.dma_st(ext, QF[seq, :, 64:67, ts0:ts0 + G], ext[:, 0:3, :], en="pool")
                    fw.dma_st(ext, KF[seq, :, 67:70, ts0:ts0 + G], ext[:, 3:6, :], en="pool")
                    pb = bm[0]
                    for tb in range(4):
                        for kc in range(KC):
                            mm(pb, pb[:, tb * 16:tb * 16 + 16], xn, xn[:, kc, tb * 128:(tb + 1) * 128], wsm, wsm[:, kc, :], kc == 0, kc == KC - 1)
                    pv = pb[:, 0:64].rearrange("p (a c) -> p a c", a=4)
                    gst = gs[g % 2]
                    op("dve", lambda e: e.tensor_tensor(tz[:, :, 0:4], pv[:, :, 8:12], pr["dtb"][:], ALU.add), reads=[pb, pr["dtb"]], writes=[tz])
                    op("dve", lambda e: e.tensor_scalar(tz[:, :, 4:8], pv[:, :, 12:16], -1.0, None, ALU.mult), reads=[pb], writes=[tz])
                    op("act", lambda e: e.activation(out=te[:], in_=tz[:], func=AF.Exp), reads=[tz], writes=[te])
                    op("act", lambda e: e.activation(out=tz[:, :, 0:4], in_=te[:, :, 0:4], func=AF.Ln, bias=onec[:], scale=1.0),
                       reads=[te, onec], writes=[tz])
                    op("dve", lambda e: e.tensor_tensor(gst[:, :, 0:4], tz[:, :, 0:4], pr["nA"][:], ALU.mult), reads=[tz, pr["nA"]], writes=[gst])
                    op("dve", lambda e: e.tensor_scalar(te[:, :, 4:8], te[:, :, 4:8], 1.0, None, ALU.add), reads=[te], writes=[te])
                    op("dve", lambda e: e.reciprocal(gst[:, :, 4:8], te[:, :, 4:8]), reads=[te], writes=[gst])
                    fw.dma_st(gst, GS[:, blk0:blk0 + 4, :], gst[:], en="pool")
            fw.barrier()

        def fox_phase():
            with contextlib.ExitStack() as st:
                Kt = [fw.sb(st, "Kt%d" % i, [P, S], BF16) for i in range(2)]
                Qt = [fw.sb(st, "Qt%d" % i, [P, S], BF16) for i in range(2)]
                chK = [fw.dma_chan("k%d" % i) for i in range(2)]
                chQ = [fw.dma_chan("q%d" % i) for i in range(2)]
                Vt = fw.sb(st, "Vt", [P, NBS, 520], BF16)
                chV = fw.dma_chan("v")
                pT = [fw.sb(st, "pT%d" % i, [P, G], BF16) for i in range(3)]
                bS = [fw.ps(st, "bS%d" % i, [P, G], F32) for i in range(3)]
                bO = [fw.ps(st, "bO%d" % i, [P, G], F32) for i in range(2)]
                bB = fw.ps(st, "bB", [P, G], F32)
                rc = fw.sb(st, "rc", [P, G], F32)
                bc = fw.sb(st, "bc", [64, G], F32)
                yo = [fw.sb(st, "yo%d" % i, [64, S], BF16) for i in range(2)]
                chO = fw.dma_chan("fo")
                fm = cb.t[:, C_FM:C_FM + 2048].rearrange("p (r t) -> p r t", r=4)
                Vv = Vt.t[:].rearrange("p a (h d) -> p a h d", h=8)
                blocks = []
                for seq in range(n_seq):
                    for h in range(8):
                        for i in range(S // G):
                            for j in range(4 * (i + 1)):
                                blocks.append((seq, h, i, j))
                hctx = {}

                def head_ctx(seq, h):
                    if (seq, h) not in hctx:
                        hi = len(hctx)
                        K_, Q_ = Kt[hi % 2], Qt[hi % 2]
                        dma(chK[hi % 2], K_[0:70, :], KF[seq, h], writes=[K_])
                        dma(chQ[hi % 2], Q_[0:70, :], QF[seq, h], writes=[Q_])
                        hctx[(seq, h)] = (K_, Q_, yo[hi % 2])
                    return hctx[(seq, h)]

                def issue_qk(n):
                    seq, h, i, j = blocks[n]
                    K_, Q_, _ = head_ctx(seq, h)
                    ps_ = bS[n % 3]
                    diag = j >= 4 * i
                    op("pe", lambda e: e.matmul(ps_[:], K_[0:70, j * 128:(j + 1) * 128], Q_[0:70, i * G:(i + 1) * G],
                                                start=True, stop=not diag), reads=[K_, Q_], writes=[ps_])
                    if diag:
                        r = j - 4 * i
                        op("pe", lambda e: e.matmul(ps_[:], ident_bf, fm[:, r, :], start=False, stop=True),
                           reads=[cb], writes=[ps_])

                def issue_rest(n):
                    seq, h, i, j = blocks[n]
                    _, _, yt = head_ctx(seq, h)
                    ps_, pt = bS[n % 3], pT[n % 3]
                    po = bO[i % 2]
                    nj = 4 * (i + 1)
                    if h == 0 and i == 0 and j == 0:
                        dma(chV, Vt[:], VF[:, seq * NBS:(seq + 1) * NBS, :], writes=[Vt])
                    op("act", lambda e: e.activation(out=pt[:], in_=ps_[:], func=AF.Exp), reads=[ps_], writes=[pt])
                    op("pe", lambda e: e.matmul(po[0:65, :], Vv[:, j, h, :], pt[:], start=(j == 0), stop=(j == nj - 1)),
                       reads=[Vt, pt], writes=[po])
                    if j == nj - 1:
                        op("dve", lambda e: e.reciprocal(rc[64:65, :], po[64:65, :]), reads=[po], writes=[rc])
                        op("pe", lambda e: e.matmul(bB[0:64, :], cf[64:65, C_ONE:C_ONE + 64], rc[64:65, :], start=True, stop=True),
                           reads=[cf, rc], writes=[bB])
                        op("act", lambda e: e.activation(out=bc[:], in_=bB[0:64, :], func=AF.Copy), reads=[bB], writes=[bc])
                        op("dve", lambda e: e.tensor_tensor(yt[:, i * G:(i + 1) * G], po[0:64, :], bc[:], ALU.mult),
                           reads=[po, bc], writes=[yt])
                        if i == S // G - 1:
                            fw.dma_st(yt, Y[h // 2, (h % 2) * 64:(h % 2) * 64 + 64, seq * S:(seq + 1) * S], yt[:])

                LA = int(os.environ.get("K_LA", "2"))
                for n in range(len(blocks) + LA):
                    if n < len(blocks):
                        issue_qk(n)
                    if n >= LA:
                        issue_rest(n - LA)
            fw.barrier()

        def gdn_phase(l):
            pr = prm[l]
            with contextlib.ExitStack() as st:
                def sbt(name, shape, dt):
                    return fw.sb(st, name, shape, dt)
                U4 = sbt("U4", [P, 4, 128], F32)
                id4 = sbt("id4", [P, 4, 128], BF16)
                nmt4 = sbt("nmt4", [P, 4, 128], BF16)
                m014 = sbt("m014", [P, 4, 128], BF16)
                for h in range(4):
                    op("dve", lambda e, h=h: e.tensor_copy(U4[:, h, :], cf[:, C_U:C_U + 128]), reads=[cf], writes=[U4])
                    op("dve", lambda e, h=h: e.tensor_copy(id4[:, h, :], cb[:, C_ID:C_ID + 128]), reads=[cb], writes=[id4])
                    op("dve", lambda e, h=h: e.tensor_copy(nmt4[:, h, :], cb[:, C_NMT:C_NMT + 128]), reads=[cb], writes=[nmt4])
                    op("dve", lambda e, h=h: e.tensor_copy(m014[:, h, :], cb[:, C_M01:C_M01 + 128]), reads=[cb], writes=[m014])
                bT = fw.ps(st, "bT", [P, 512], BF16)
                bA, bB_, bC, bD, bE, bF, bG_ = [fw.ps(st, "gb%d" % i, [P, 512], F32) for i in range(7)]
                ones_f = cf.t[:, C_ONE:C_ONE + 128]

                def v4(tk):
                    return tk.t[:].rearrange("p (h i) -> p h i", h=4)

                def mmh(out_tk, l_tk, l_fn, r_tk, r_fn, grp=True):
                    for h in range(4):
                        op("pe", lambda e, h=h: e.matmul(v4(out_tk)[:, h, :], l_fn(h), r_fn(h), start=True, stop=True),
                           reads=[l_tk, r_tk], writes=[out_tk])


                def make_seq(seq):
                    sfx = "s%d_" % seq
                    qkv = [[sbt(sfx + "%s%d" % (nm, i), [P, 4, G], BF16) for nm in ("gq", "gk", "gv", "gg")] for i in range(2)]
                    chq = [fw.dma_chan(sfx + "g%d" % i) for i in range(2)]
                    gsb = sbt(sfx + "gsb", [P, NBS, 8], F32)
                    chg = fw.dma_chan(sfx + "gs")
                    ng = sbt(sfx + "ng", [P, 4], F32)
                    nbeta = sbt(sfx + "nbeta", [P, 4], F32)
                    NG1 = sbt(sfx + "NG1", [P, 4, 128], F32)
                    esm = sbt(sfx + "esm", [P, 12], F32)
                    E = sbt(sfx + "E", [P, 4, 128], BF16)
                    qdT = sbt(sfx + "qdT", [P, 4, 128], BF16)
                    DT = sbt(sfx + "DT", [P, 4, 128], BF16)
                    DTs = sbt(sfx + "DTs", [P, 4, 128], BF16)
                    attnT = sbt(sfx + "attnT", [P, 4, 128], BF16)
                    Zc = [sbt(sfx + "Zc%d" % i, [P, 4, 128], BF16) for i in range(2)]
                    ZAc = [sbt(sfx + "ZAc%d" % i, [P, 4, 128], BF16) for i in range(2)]
                    Pc = [sbt(sfx + "Pc%d" % i, [P, 4, 128], BF16) for i in range(2)]
                    kg = sbt(sfx + "kg", [P, 4, 128], BF16)
                    kd = sbt(sfx + "kd", [P, 4, 128], BF16)
                    vtk = sbt(sfx + "vtk", [P, 4, 128], BF16)
                    ktk = sbt(sfx + "ktk", [P, 4, 128], BF16)
                    nwT = sbt(sfx + "nwT", [P, 4, 128], BF16)
                    vn = sbt(sfx + "vn", [P, 4, 128], BF16)
                    S32 = sbt(sfx + "S32", [P, 4, 128], F32)
                    Sb = sbt(sfx + "Sb", [P, 4, 128], BF16)
                    osq = sbt(sfx + "osq", [P, 4, 128], BF16)
                    o32 = sbt(sfx + "o32", [P, 4, 128], F32)
                    msn = sbt(sfx + "msn", [P, 512], F32)
                    rsn = sbt(sfx + "rsn", [P, 512], F32)
                    y1 = sbt(sfx + "y1", [P, 4, 128], F32)
                    ybuf = [sbt(sfx + "ybuf%d" % i, [P, 4, G], BF16) for i in range(2)]

                    stt = {"gi": 0, "cur": None, "yb": None}

                    def chunk(c):
                        if c == 0:
                            dma(chg, gsb[:], GS[:, seq * NBS:(seq + 1) * NBS, :], writes=[gsb])
                            op("dve", lambda e: e.memset(S32[:], 0.0), writes=[S32])
                            op("dve", lambda e: e.memset(Sb[:], 0.0), writes=[Sb])
                        tok0 = seq * S + c * P
                        if c % 4 == 0:
                            gi = stt["gi"]
                            stt["cur"] = qkv[gi % 2]
                            stt["yb"] = ybuf[gi % 2]
                            for t_, src_ in zip(stt["cur"], (GQ, GK, GV, GGs)):
                                dma(chq[gi % 2], t_[:], src_[:, :, tok0:tok0 + G].rearrange("h p t -> p h t"), writes=[t_])
                            for t_ in stt["cur"]:
                                t_.w = (chq[gi % 2], chq[gi % 2].count)
                            stt["gi"] = gi + 1
                        cur, yb = stt["cur"], stt["yb"]
                        qT_, kT_, vT_, gg_ = cur
                        co = (c % 4) * P
                        g_ap = gsb[:, c, 0:4]
                        b_ap = gsb[:, c, 4:8]
                        op("dve", lambda e: e.tensor_scalar(ng[:], g_ap, -1.0, None, ALU.mult), reads=[gsb], writes=[ng])
                        op("dve", lambda e: e.tensor_scalar(nbeta[:], b_ap, -1.0, None, ALU.mult), reads=[gsb], writes=[nbeta])
                        op("dve", lambda e: e.tensor_tensor(NG1[:], U4[:], ng[:].unsqueeze(2).to_broadcast([P, 4, 128]), ALU.mult),
                           reads=[U4, ng], writes=[NG1])
                        op("pe", lambda e: e.matmul(bA[:, 0:4], cf[:, C_U:C_U + 128], g_ap, start=True, stop=True), reads=[cf, gsb], writes=[bA])
                        op("pe", lambda e: e.matmul(bA[:, 4:8], cf[:, C_L:C_L + 128], g_ap, start=True, stop=True), reads=[cf, gsb], writes=[bA])
                        op("pe", lambda e: e.matmul(bA[:, 8:12], ones_f, g_ap, start=True, stop=True), reads=[cf, gsb], writes=[bA])
                        op("act", lambda e: e.activation(out=esm[:], in_=bA[:, 0:12], func=AF.Exp), reads=[bA], writes=[esm])
                        NG1f = NG1.t[:].rearrange("p h i -> p (h i)")
                        op("pe", lambda e: e.matmul(bB_[:], negones[:], NG1f, start=True, stop=True), reads=[negones, NG1], writes=[bB_])
                        op("act", lambda e: e.activation(out=E[:].rearrange("p h i -> p (h i)"), in_=bB_[:], func=AF.Exp), reads=[bB_], writes=[E])
                        op("dve", lambda e: e.tensor_tensor(qdT[:], qT_[:, :, co:co + P], E[:], ALU.mult), reads=[qT_, E], writes=[qdT])
                        op("pe", lambda e: e.matmul(bC[:], negones[:], NG1f, start=True, stop=False), reads=[negones, NG1], writes=[bC])
                        for h in range(4):
                            op("pe", lambda e, h=h: e.matmul(v4(bC)[:, h, :], NG1[:, h, :], ones_f, start=False, stop=False),
                               reads=[NG1, cf], writes=[bC])
                        op("pe", lambda e: e.matmul(bC[:], ident_bf, nmt4[:].rearrange("p h i -> p (h i)"), start=False, stop=True),
                           reads=[cb, nmt4], writes=[bC])
                        op("act", lambda e: e.activation(out=DT[:].rearrange("p h i -> p (h i)"), in_=bC[:], func=AF.Exp), reads=[bC], writes=[DT])
                        op("pool", lambda e: e.tensor_tensor(DTs[:], DT[:], m014[:], ALU.mult), reads=[DT, m014], writes=[DTs])
                        yield
                        mmh(bD, kT_, lambda h: kT_[:, h, co:co + P], kT_, lambda h: kT_[:, h, co:co + P])
                        mmh(bE, kT_, lambda h: kT_[:, h, co:co + P], qT_, lambda h: qT_[:, h, co:co + P])
                        Z, ZA, Pm = Zc[0], ZAc[0], Pc[0]
                        for h in range(4):
                            op("dve", lambda e, h=h, Z=Z: e.scalar_tensor_tensor(
                                out=Z[:, h, :], in0=v4(bD)[:, h, :], scalar=nbeta[:, h:h + 1], in1=DTs[:, h, :],
                                op0=ALU.mult, op1=ALU.mult), reads=[bD, nbeta, DTs], writes=[Z])
                        op("dve", lambda e: e.tensor_tensor(attnT[:].rearrange("p h i -> p (h i)"), bE[:], DT[:].rearrange("p h i -> p (h i)"), ALU.mult),
                           reads=[bE, DT], writes=[attnT])
                        op("pool", lambda e, Z=Z, Pm=Pm: e.tensor_tensor(Pm[:], Z[:], id4[:], ALU.add), reads=[Z, id4], writes=[Pm])
                        bTv = bT.t[:, 0:512].rearrange("p (h i) -> p h i", h=4)
                        for h in range(4):
                            op("pe", lambda e, h=h, Z=Z: e.transpose(bTv[:, h, :], Z[:, h, :], ident_bf), reads=[Z, cb], writes=[bT])
                        op("act", lambda e, ZA=ZA: e.activation(out=ZA[:], in_=bTv, func=AF.Copy), reads=[bT], writes=[ZA])
                        yield
                        for m in range(1, 7):
                            Zn, ZAn, Pn = Zc[m % 2], ZAc[m % 2], Pc[m % 2]
                            mmh(bF, Z, lambda h, Z=Z: Z[:, h, :], ZA, lambda h, ZA=ZA: ZA[:, h, :])
                            op("act", lambda e, ZAn=ZAn: e.activation(out=ZAn[:].rearrange("p h i -> p (h i)"), in_=bF[:], func=AF.Copy),
                               reads=[bF], writes=[ZAn])
                            if m <= 5:
                                mmh(bG_, ZA, lambda h, ZA=ZA: ZA[:, h, :], Z, lambda h, Z=Z: Z[:, h, :])
                                op("dve", lambda e, Zn=Zn: e.tensor_copy(Zn[:].rearrange("p h i -> p (h i)"), bG_[:]), reads=[bG_], writes=[Zn])
                            mmh(bD, ZAn, lambda h, ZAn=ZAn: ZAn[:, h, :], Pm, lambda h, Pm=Pm: Pm[:, h, :])
                            op("dve", lambda e, Pn=Pn, Pm=Pm: e.tensor_tensor(Pn[:].rearrange("p h i -> p (h i)"), bD[:], Pm[:].rearrange("p h i -> p (h i)"), ALU.add),
                               reads=[bD, Pm], writes=[Pn])
                            Z, ZA, Pm = Zn, ZAn, Pn
                            yield
                        Tt = Pm
                        bTk = bT.t[:, 0:512].rearrange("p (h i) -> p h i", h=4)
                        for h in range(4):
                            op("pe", lambda e, h=h: e.transpose(bTk[:, h, :], kT_[:, h, co:co + P], ident_bf), reads=[kT_, cb], writes=[bT])
                        op("act", lambda e: e.activation(out=ktk[:], in_=bTk, func=AF.Copy), reads=[bT], writes=[ktk])
                        for h in range(4):
                            op("pe", lambda e, h=h: e.transpose(bTk[:, h, :], vT_[:, h, co:co + P], ident_bf), reads=[vT_, cb], writes=[bT])
                        op("act", lambda e: e.activation(out=vtk[:], in_=bTk, func=AF.Copy), reads=[bT], writes=[vtk])
                        op("dve", lambda e: e.tensor_tensor(kg[:], ktk[:], esm[:, 0:4].unsqueeze(2).to_broadcast([P, 4, 128]), ALU.mult),
                           reads=[ktk, esm], writes=[kg])
                        op("dve", lambda e: e.tensor_tensor(kd[:], ktk[:], esm[:, 4:8].unsqueeze(2).to_broadcast([P, 4, 128]), ALU.mult),
                           reads=[ktk, esm], writes=[kd])
                        mmh(bB_, kg, lambda h: kg[:, h, :], Tt, lambda h, Tt=Tt: Tt[:, h, :])
                        op("act", lambda e: e.activation(out=nwT[:].rearrange("p h i -> p (h i)"), in_=bB_[:], func=AF.Identity, scale=-1.0),
                           reads=[bB_], writes=[nwT])
                        yield
                        for h in range(4):
                            op("pe", lambda e, h=h, Tt=Tt: e.matmul(v4(bC)[:, h, :], Tt[:, h, :], vtk[:, h, :], start=True, stop=False),
                               reads=[Tt, vtk], writes=[bC])
                            op("pe", lambda e, h=h: e.matmul(v4(bC)[:, h, :], nwT[:, h, :], Sb[:, h, :], start=False, stop=True),
                               reads=[nwT, Sb], writes=[bC])
                        op("dve", lambda e: e.tensor_tensor(vn[:], v4(bC), b_ap.unsqueeze(2).to_broadcast([P, 4, 128]), ALU.mult),
                           reads=[bC, gsb], writes=[vn])
                        for h in range(4):
                            op("pe", lambda e, h=h: e.matmul(v4(bE)[:, h, :], Sb[:, h, :], qdT[:, h, :], start=True, stop=False),
                               reads=[Sb, qdT], writes=[bE])
                            op("pe", lambda e, h=h: e.matmul(v4(bE)[:, h, :], vn[:, h, :], attnT[:, h, :], start=False, stop=True),
                               reads=[vn, attnT], writes=[bE])
                        mmh(bF, kd, lambda h: kd[:, h, :], vn, lambda h: vn[:, h, :])
                        op("dve", lambda e: e.tensor_tensor(S32[:], S32[:], esm[:, 8:12].unsqueeze(2).to_broadcast([P, 4, 128]), ALU.mult),
                           reads=[S32, esm], writes=[S32])
                        op("dve", lambda e: e.tensor_tensor(S32[:].rearrange("p h i -> p (h i)"), bF[:], S32[:].rearrange("p h i -> p (h i)"), ALU.add),
                           reads=[bF, S32], writes=[S32])
                        op("act", lambda e: e.activation(out=Sb[:], in_=S32[:], func=AF.Copy), reads=[S32], writes=[Sb])
                        op("act", lambda e: e.activation(out=osq[:].rearrange("p h i -> p (h i)"), in_=bE[:], func=AF.Square), reads=[bE], writes=[osq])
                        op("act", lambda e: e.activation(out=o32[:].rearrange("p h i -> p (h i)"), in_=bE[:], func=AF.Copy), reads=[bE], writes=[o32])
                        op("pe", lambda e: e.matmul(bG_[:], ones_bf, osq[:].rearrange("p h i -> p (h i)"), start=True, stop=True), reads=[cb, osq], writes=[bG_])
                        op("act", lambda e: e.activation(out=msn[:], in_=bG_[:], func=AF.Ln, bias=epsc[:], scale=1.0 / 128),
                           reads=[bG_, epsc], writes=[msn])
                        op("act", lambda e: e.activation(out=rsn[:], in_=msn[:], func=AF.Exp, scale=-0.5), reads=[msn], writes=[rsn])
                        op("dve", lambda e: e.scalar_tensor_tensor(out=y1[:].rearrange("p h i -> p (h i)"), in0=o32[:].rearrange("p h i -> p (h i)"),
                                                                   scalar=pr["gon"][:, 0:1], in1=rsn[:], op0=ALU.mult, op1=ALU.mult),
                           reads=[o32, pr["gon"], rsn], writes=[y1])
                        op("dve", lambda e, yb=yb: e.tensor_tensor(yb[:, :, co:co + P], y1[:], gg_[:, :, co:co + P], ALU.mult),
                           reads=[y1, gg_], writes=[yb])
                        if c % 4 == 3:
                            tb0 = tok0 - 3 * P
                            fw.dma_st(yb, Y[4:8, :, tb0:tb0 + G].rearrange("h p t -> p h t"), yb[:])

                    return chunk

                chunk_fns = [make_seq(sq_) for sq_ in range(n_seq)]
                for c in range(NBS):
                    gens = [f(c) for f in chunk_fns]
                    while gens:
                        for g_ in list(gens):
                            try:
                                next(g_)
                            except StopIteration:
                                gens.remove(g_)

            fw.barrier()

        import os
        stop = int(os.environ.get("K_STOP", "99"))
        for l in range(depth):
            if stop <= 0:
                break
            if l == 0:
                group_pass(xT, None, None, (1, 0), 0, False)
            else:
                group_pass(XR, l - 1, (2, l - 1), (1, l), l, False)
            if stop <= 1:
                break
            fox_phase()
            if stop <= 2:
                break
            gdn_phase(l)
        if stop > 3:
            group_pass(XR, depth - 1, (2, depth - 1), None, None, True)
    return nc


def host_inputs(inputs, n_cores, n_seq, S):
    consts = make_consts()
    in_maps = []
    x = np.asarray(inputs["x"], np.float32)
    depth = int(np.asarray(inputs["w_in"]).shape[0])
    big = ("ffn1_w_in", "ffn1_w_out", "ffn2_w_in", "ffn2_w_out", "w_in", "w_out")
    shared = {k: np.ascontiguousarray(np.asarray(inputs[k], np.float32)) for k in big}
    shared["pvec"] = make_pvec(inputs, depth)
    for c in range(n_cores):
        xs = x[c * n_seq:(c + 1) * n_seq].reshape(n_seq * S, D)
        xT = np.ascontiguousarray(xs.T).reshape(KC, P, n_seq * S)
        m = {"xT": xT, "consts": consts}
        m.update(shared)
        in_maps.append(m)
    return in_maps


def kernel(**inputs):
    x = np.asarray(inputs["x"])
    B, S, _ = x.shape
    depth = int(np.asarray(inputs["w_in"]).shape[0])
    n_seq = B // N_CORES
    nc = bass.Bass("TRN2", target_bir_lowering=False)
    build(nc, n_seq, S, depth)
    in_maps = host_inputs(inputs, N_CORES, n_seq, S)
    res = run_bass_kernel_spmd(nc, in_maps, core_ids=list(range(N_CORES)))
    out = np.empty((B, S, D), np.float32)
    for c in range(N_CORES):
        oT = np.asarray(res.results[c]["outT"]).reshape(D, n_seq * S)
        out[c * n_seq:(c + 1) * n_seq] = oT.T.reshape(n_seq, S, D)
    return out
```
